# Optimizing a Trainium2 kernel written in Bass

```python
import math
import jax, jax.numpy as jnp
from jax import lax
import numpy as np

D_MODEL = 4096
BATCH = 4
SEQ = 2048
DEPTH = 1
DEC_BATCH = 32
DEC_SEQ = 16
PAST_LEN = 4096

CHUNK = 64
W_CONV = D_MODEL // 2
W_LRU = D_MODEL // 2
N_LRU_HEADS = 16
LRU_HEAD_DIM = W_LRU // N_LRU_HEADS
K_CONFORMER = 31
K_LRU = 4
K_FFN = 3
D_FF = ((8 * D_MODEL // 3 + 255) // 256) * 256
LRU_C = 8.0
EPS = 1e-6
D_IN = 2 * W_CONV + 2 * W_LRU + 2 * D_MODEL

kernel_name = 'hybrid_conformer_rglru_stream_step'


def rmsnorm(x, g):
    xf = x.astype(jnp.float32)
    y = xf * lax.rsqrt(jnp.mean(xf * xf, axis=-1, keepdims=True) + EPS)
    return (y * g.astype(jnp.float32)).astype(x.dtype)


def layernorm(x, g, b):
    xf = x.astype(jnp.float32)
    mu = jnp.mean(xf, axis=-1, keepdims=True)
    var = jnp.mean(jnp.square(xf - mu), axis=-1, keepdims=True)
    y = (xf - mu) * lax.rsqrt(var + EPS) * g.astype(jnp.float32) + b.astype(jnp.float32)
    return y.astype(x.dtype)


def causal_dwconv(x, buf, w, b):
    k = w.shape[0]
    xp = jnp.concatenate([buf.astype(x.dtype), x], axis=1)
    y = lax.conv_general_dilated(xp, w[:, None, :].astype(x.dtype), window_strides=(1,),
                                 padding='VALID', dimension_numbers=('NWC', 'WIO', 'NWC'),
                                 feature_group_count=x.shape[-1])
    new_buf = xp[:, xp.shape[1] - (k - 1):]
    return y + b.astype(x.dtype), new_buf.astype(buf.dtype)


def _lin_combine(left, right):
    a1, b1 = left
    a2, b2 = right
    return a1 * a2, a2 * b1 + b2


def rg_lru(x, h0, w_r, b_r, w_i, b_i, lam):
    B, T, W = x.shape
    xh = x.reshape(B, T, N_LRU_HEADS, LRU_HEAD_DIM)
    r = jax.nn.sigmoid(jnp.einsum('bthd,hde->bthe', xh, w_r).reshape(B, T, W).astype(jnp.float32)
                       + b_r.astype(jnp.float32))
    i = jax.nn.sigmoid(jnp.einsum('bthd,hde->bthe', xh, w_i).reshape(B, T, W).astype(jnp.float32)
                       + b_i.astype(jnp.float32))
    log_a = -LRU_C * r * jax.nn.softplus(-lam.astype(jnp.float32))
    a = jnp.exp(log_a)
    u = jnp.sqrt(-jnp.expm1(2.0 * log_a)) * (i * x.astype(jnp.float32))
    blk = math.gcd(T, CHUNK)
    n = T // blk
    a_c = a.reshape(B, n, blk, W)
    u_c = u.reshape(B, n, blk, W)
    a_cum, u_cum = lax.associative_scan(_lin_combine, (a_c, u_c), axis=2)

    def step(h, inp):
        ac, uc = inp
        hs = ac * h[:, None, :] + uc
        return hs[:, -1], hs

    h_last, hs = lax.scan(step, h0.astype(jnp.float32),
                          (jnp.moveaxis(a_cum, 1, 0), jnp.moveaxis(u_cum, 1, 0)))
    y = jnp.moveaxis(hs, 0, 1).reshape(B, T, W)
    return y.astype(x.dtype), h_last.astype(h0.dtype)


def layer(x, s_conv_a, s_conv_b, s_lru, s_ffn,
          g_pre_mix, w_in, w_dw_a, b_dw_a, ln_a_g, ln_a_b, w_a_out,
          w_dw_b, b_dw_b, w_rg_r, b_rg_r, w_rg_i, b_rg_i, lru_lambda, w_b_out,
          w_o, g_post_mix, g_pre_ffn, w_up, w_dw_f, b_dw_f, w_down, g_post_ffn):
    h = rmsnorm(x, g_pre_mix)
    z = h @ w_in
    o1 = W_CONV
    o2 = o1 + W_CONV
    o3 = o2 + W_LRU
    o4 = o3 + W_LRU
    o5 = o4 + D_MODEL
    a_val, a_gate = z[..., :o1], z[..., o1:o2]
    b_x, b_gate = z[..., o2:o3], z[..., o3:o4]
    m_a, m_b = z[..., o4:o5], z[..., o5:]
    ua = a_val * jax.nn.sigmoid(a_gate)
    ua, new_sa = causal_dwconv(ua, s_conv_a, w_dw_a, b_dw_a)
    ua = jax.nn.silu(layernorm(ua, ln_a_g, ln_a_b))
    out_a = ua @ w_a_out
    xb, new_sb = causal_dwconv(b_x, s_conv_b, w_dw_b, b_dw_b)
    hb, new_h = rg_lru(xb, s_lru, w_rg_r, b_rg_r, w_rg_i, b_rg_i, lru_lambda)
    out_b = (jax.nn.gelu(b_gate) * hb) @ w_b_out
    mixed = jax.nn.sigmoid(m_a) * out_a + jax.nn.sigmoid(m_b) * out_b
    x = x + rmsnorm(mixed @ w_o, g_post_mix)
    h2 = rmsnorm(x, g_pre_ffn)
    up = h2 @ w_up
    up, new_sf = causal_dwconv(up, s_ffn, w_dw_f, b_dw_f)
    f = (jax.nn.gelu(up[..., :D_FF]) * up[..., D_FF:]) @ w_down
    x = x + rmsnorm(f, g_post_ffn)
    return x, new_sa, new_sb, new_h, new_sf


def setup_inputs(seed: int = 0) -> dict:
    key = jax.random.key(seed)
    ks = jax.random.split(key, 32)

    def nrm(k, shape, scale):
        return jax.random.normal(k, shape, jnp.float32) * scale

    L = DEPTH
    a0 = jax.random.uniform(ks[20], (L, W_LRU), jnp.float32, 0.9, 0.999)
    a_base = a0 ** (1.0 / LRU_C)
    lru_lambda = jnp.log(a_base) - jnp.log1p(-a_base)
    return {
        'x_prompt': nrm(ks[0], (BATCH, SEQ, D_MODEL), 1.0),
        'x_sample': nrm(ks[1], (DEC_BATCH, DEC_SEQ, D_MODEL), 1.0),
        'state_conv_a': nrm(ks[2], (L, DEC_BATCH, K_CONFORMER - 1, W_CONV), 1.0),
        'state_conv_b': nrm(ks[3], (L, DEC_BATCH, K_LRU - 1, W_LRU), 1.0),
        'state_lru': nrm(ks[4], (L, DEC_BATCH, W_LRU), 0.5),
        'state_ffn': nrm(ks[5], (L, DEC_BATCH, K_FFN - 1, 2 * D_FF), 1.0),
        'g_pre_mix': 1.0 + nrm(ks[6], (L, D_MODEL), 0.02),
        'w_in': nrm(ks[7], (L, D_MODEL, D_IN), D_MODEL ** -0.5),
        'w_dw_a': nrm(ks[8], (L, K_CONFORMER, W_CONV), K_CONFORMER ** -0.5),
        'b_dw_a': nrm(ks[9], (L, W_CONV), 0.01),
        'ln_a_g': 1.0 + nrm(ks[10], (L, W_CONV), 0.02),
        'ln_a_b': nrm(ks[11], (L, W_CONV), 0.01),
        'w_a_out': nrm(ks[12], (L, W_CONV, D_MODEL), W_CONV ** -0.5),
        'w_dw_b': nrm(ks[13], (L, K_LRU, W_LRU), K_LRU ** -0.5),
        'b_dw_b': nrm(ks[14], (L, W_LRU), 0.01),
        'w_rg_r': nrm(ks[15], (L, N_LRU_HEADS, LRU_HEAD_DIM, LRU_HEAD_DIM), LRU_HEAD_DIM ** -0.5),
        'b_rg_r': nrm(ks[16], (L, W_LRU), 0.01),
        'w_rg_i': nrm(ks[17], (L, N_LRU_HEADS, LRU_HEAD_DIM, LRU_HEAD_DIM), LRU_HEAD_DIM ** -0.5),
        'b_rg_i': nrm(ks[18], (L, W_LRU), 0.01),
        'lru_lambda': lru_lambda,
        'w_b_out': nrm(ks[19], (L, W_LRU, D_MODEL), W_LRU ** -0.5),
        'w_o': nrm(ks[21], (L, D_MODEL, D_MODEL), D_MODEL ** -0.5),
        'g_post_mix': 1.0 + nrm(ks[22], (L, D_MODEL), 0.02),
        'g_pre_ffn': 1.0 + nrm(ks[23], (L, D_MODEL), 0.02),
        'w_up': nrm(ks[24], (L, D_MODEL, 2 * D_FF), D_MODEL ** -0.5),
        'w_dw_f': nrm(ks[25], (L, K_FFN, 2 * D_FF), K_FFN ** -0.5),
        'b_dw_f': nrm(ks[26], (L, 2 * D_FF), 0.01),
        'w_down': nrm(ks[27], (L, D_FF, D_MODEL), D_FF ** -0.5),
        'g_post_ffn': 1.0 + nrm(ks[28], (L, D_MODEL), 0.02),
    }


def reference(x_prompt, x_sample, state_conv_a, state_conv_b, state_lru, state_ffn,
              g_pre_mix, w_in, w_dw_a, b_dw_a, ln_a_g, ln_a_b, w_a_out,
              w_dw_b, b_dw_b, w_rg_r, b_rg_r, w_rg_i, b_rg_i, lru_lambda, w_b_out,
              w_o, g_post_mix, g_pre_ffn, w_up, w_dw_f, b_dw_f, w_down, g_post_ffn):
    dt = x_prompt.dtype
    yp = x_prompt
    ys = x_sample
    pa, pb, ph, pf = [], [], [], []
    sa, sb, sh, sf = [], [], [], []
    for l in range(DEPTH):
        w_l = (g_pre_mix[l], w_in[l], w_dw_a[l], b_dw_a[l], ln_a_g[l], ln_a_b[l], w_a_out[l],
               w_dw_b[l], b_dw_b[l], w_rg_r[l], b_rg_r[l], w_rg_i[l], b_rg_i[l], lru_lambda[l],
               w_b_out[l], w_o[l], g_post_mix[l], g_pre_ffn[l], w_up[l], w_dw_f[l], b_dw_f[l],
               w_down[l], g_post_ffn[l])
        z_a = jnp.zeros((BATCH, K_CONFORMER - 1, W_CONV), dt)
        z_b = jnp.zeros((BATCH, K_LRU - 1, W_LRU), dt)
        z_h = jnp.zeros((BATCH, W_LRU), dt)
        z_f = jnp.zeros((BATCH, K_FFN - 1, 2 * D_FF), dt)
        yp, n_a, n_b, n_h, n_f = layer(yp, z_a, z_b, z_h, z_f, *w_l)
        pa.append(n_a); pb.append(n_b); ph.append(n_h); pf.append(n_f)
        ys, m_a, m_b, m_h, m_f = layer(ys, state_conv_a[l], state_conv_b[l], state_lru[l],
                                       state_ffn[l], *w_l)
        sa.append(m_a); sb.append(m_b); sh.append(m_h); sf.append(m_f)
    return (yp, ys,
            jnp.stack(pa), jnp.stack(pb), jnp.stack(ph), jnp.stack(pf),
            jnp.stack(sa), jnp.stack(sb), jnp.stack(sh), jnp.stack(sf))
```

```python
import os
from contextlib import ExitStack
import numpy as np
import concourse.bass as bass
import concourse.mybir as mybir
from concourse.bass_utils import run_bass_kernel_spmd

F32 = mybir.dt.float32
BF16 = mybir.dt.bfloat16
AF = mybir.ActivationFunctionType
ALU = mybir.AluOpType

NCORES = 8
D = 4096
KB = 32
E = 576
FO = 30
NF = E - FO
PE_ = 544
NB = 544
TP = 512
NFF = 86
EPS = 1e-6

P_GPRE = 0
P_WA = 32
P_BA = 528
P_LG = 544
P_LB = 560
P_WB = 576
P_BB = 640
P_BR = 656
P_BI = 672
P_LAM = 688
P_GPM = 704
P_GPF = 736
P_WF = 768
P_BF = 1284
P_GPO = 1456
NPAR = 1488

CID_AB = 128
CID_WO = 160
CID_UP = 192
CID_DN = 364
NCH = 460

SAME_SYNC = os.environ.get("MK_SAME_SYNC", "1") == "1"
NWS = 3
SAME_LAG = int(os.environ.get("MK_SAME_LAG", "1000000"))
NPT = 4
PTS = 1024


class Sched:
    def __init__(self, nc, es, kdma=8):
        self.nc = nc
        self.streams = {e: [] for e in ("pe", "act", "dve", "pool", "sp")}
        self.sems = []
        self.csem = {}
        self.cnt = {}
        for e in ("pe", "act", "dve", "pool"):
            self.csem[e] = self._newsem(es, "c_" + e)
            self.cnt[e] = 0
        self.dq = {}
        for e in ("sp", "pool"):
            self.dq[e] = {"keys": [self._newsem(es, f"d_{e}{i}") for i in range(kdma)], "n": 0}
        self.kdma = kdma
        self.seen = {e: {} for e in self.streams}
        self.last_w = {}
        self.readers = {}

    def _newsem(self, es, name):
        self.sems.append(es.enter_context(self.nc.semaphore(name)))
        return len(self.sems) - 1

    def _waits(self, eng, deps):
        need = {}
        for (k, v) in deps:
            if v > need.get(k, 0):
                need[k] = v
        for k, v in need.items():
            if eng in self.csem and k == self.csem[eng]:
                if (not SAME_SYNC) or eng == "pe":
                    continue
                if self.cnt[eng] - v >= SAME_LAG:
                    continue
            if self.seen[eng].get(k, 0) >= v:
                continue
            self.seen[eng][k] = v
            sem = self.sems[k]
            self.streams[eng].append(lambda h, sem=sem, v=v: h.wait_ge(sem, v))

    def op(self, eng, fn, R=(), W=(), dma=False):
        W = list(W) + [r for r in R if r.startswith("ps")]
        R = [r for r in R if not r.startswith("ps")]
        deps = []
        for r in R:
            t = self.last_w.get(r)
            if t:
                deps.append(t)
        for w in W:
            t = self.last_w.get(w)
            if t:
                deps.append(t)
            deps.extend(self.readers.get(w, {}).items())
        if dma:
            q = self.dq[eng]
            i = q["n"]
            q["n"] += 1
            k = q["keys"][i % self.kdma]
            val = 16 * (i // self.kdma + 1)
            if val > 16:
                deps.append((k, val - 16))
            tok = (k, val)
            inc = 16
        else:
            self.cnt[eng] += 1
            tok = (self.csem[eng], self.cnt[eng])
            inc = 1
        self._waits(eng, deps)
        sem = self.sems[tok[0]]
        self.streams[eng].append(lambda h, fn=fn, sem=sem, inc=inc: fn(h).then_inc(sem, inc))
        if (not dma):
            pass
        for r in R:
            d = self.readers.setdefault(r, {})
            if tok[1] > d.get(tok[0], 0):
                d[tok[0]] = tok[1]
        for w in W:
            self.last_w[w] = tok
            self.readers[w] = {}
        return tok

    def final_wait(self, eng):
        deps = []
        for e, k in self.csem.items():
            if self.cnt[e] > 0:
                deps.append((k, self.cnt[e]))
        for e, q in self.dq.items():
            for j, k in enumerate(q["keys"]):
                n = q["n"]
                cntj = (n - j + self.kdma - 1) // self.kdma if n > j else 0
                if cntj > 0:
                    deps.append((k, 16 * cntj))
        save = SAME_SYNC
        need = {}
        for (k, v) in deps:
            need[k] = max(need.get(k, 0), v)
        for k, v in need.items():
            if self.seen[eng].get(k, 0) >= v:
                continue
            self.seen[eng][k] = v
            sem = self.sems[k]
            self.streams[eng].append(lambda h, sem=sem, v=v: h.wait_ge(sem, v))


def pieces(tile, lo, hi):
    out = []
    base = tile * PTS
    c = lo
    while c < hi:
        flat = base + c
        nxt = min(hi, c + (512 - flat % 512))
        out.append((c, nxt, flat))
        c = nxt
    return out


def gen_plan():
    plan = []
    for r in range(2):
        for j in range(16):
            plan.append((32 + j, 32))
    for hi in range(2):
        for j in range(16):
            plan.append((j, 32))
            plan.append((16 + j, 32))
        for j in range(16):
            plan.append((32 + j, 32))
            plan.append((48 + j, 32))
        for j in range(32):
            plan.append((CID_AB + j, 32))
            plan.append((64 + j, 32))
            plan.append((96 + j, 32))
        for j in range(32):
            plan.append((CID_WO + j, 32))
        for m in range(NFF):
            plan.append((CID_UP + m, 32))
            plan.append((CID_UP + NFF + m, 32))
        for j in range(32):
            plan.append((CID_DN + 3 * j, 32))
            plan.append((CID_DN + 3 * j + 1, 32))
            plan.append((CID_DN + 3 * j + 2, 22))
    return plan


class _Stop(Exception):
    pass


STAGE = int(os.environ.get("MK_STAGE", "99"))


def build_program(plan=None):
    if plan is None:
        req = []
        build_program(plan=req)
        plan_final = [tuple(x) for x in req if x[0] != "end"]
        return build_program(plan=plan_final + [("end",)]), plan_final
    dry = not (len(plan) > 0 and plan[-1] == ("end",))
    if not dry:
        plan = plan[:-1]
    nch = NCH if dry else (max(c for c, _ in plan) + 1 if plan else 1)
    nc = bass.Bass("TRN2", target_bir_lowering=False)
    dt = nc.dram_tensor
    xm = dt("xm", [2, KB, 128, E], F32, kind="ExternalInput").ap()
    xp = dt("xp", [2, KB, 128, TP], F32, kind="ExternalInput").ap()
    par_d = dt("par", [128, NPAR], F32, kind="ExternalInput").ap()
    msk_d = dt("msk", [128, 2], F32, kind="ExternalInput").ap()
    wg_d = dt("wg", [16, 128, 256], F32, kind="ExternalInput").ap()
    wall = dt("wall", [nch, 128, 4096], F32, kind="ExternalInput").ap()
    sa_d = dt("sa", [2, 128, 960], F32, kind="ExternalInput").ap()
    sb_d = dt("sbv", [128, 192], F32, kind="ExternalInput").ap()
    sh_d = dt("sh", [128, 64], F32, kind="ExternalInput").ap()
    sf_d = dt("sf", [128, 2 * 172 * 4], F32, kind="ExternalInput").ap()
    y_d = dt("y", [2, KB, 128, NB], F32, kind="ExternalOutput").ap()
    oca_d = dt("oca", [2, 16, 128, 90], F32, kind="ExternalOutput").ap()
    ocb_d = dt("ocb", [128, 288], F32, kind="ExternalOutput").ap()
    olr_d = dt("olr", [128, 96], F32, kind="ExternalOutput").ap()
    off_d = dt("off", [128, 2 * 172 * 6], F32, kind="ExternalOutput").ap()
    xmid_d = dt("xmid_s", [2, KB, 128, NF], F32).ap()
    f_d = dt("f_s", [2, KB, 128, NF], F32).ap()

    es = ExitStack()
    with es:
        def sbt(name, shape, dtype):
            return es.enter_context(nc.sbuf_tensor(name, shape, dtype))

        AR1 = sbt("AR1", [128, 48128], BF16)
        AR3 = sbt("AR3", [128, KB * NF], BF16)
        WS = [sbt(f"ws{i}", [128, 4096], BF16) for i in range(NWS)]
        par = sbt("par_s", [128, NPAR], F32)
        der = sbt("der", [128, 64], F32)
        msk = sbt("msk_s", [128, 2], F32)
        ones = sbt("ones", [128, 128], BF16)
        hl16 = sbt("hl16", [128, 2 * E], BF16)
        epsb = sbt("epsb", [128, 2], F32)
        sbv = sbt("sbv_s", [128, 192], F32)
        shs = sbt("sh_s", [128, 64], F32)
        sfs = sbt("sf_s", [128, 2 * 172 * 4], F32)
        ocb = sbt("ocb_s", [128, 288], F32)
        olr = sbt("olr_s", [128, 96], F32)
        off = sbt("off_s", [128, 2 * 172 * 6], F32)
        bxst = sbt("bxst", [128, 48], F32)
        hbhl = sbt("hbhl", [128, 32], F32)
        h0b = sbt("h0b", [128, 16], F32)
        hraw = sbt("hraw", [128, 16], F32)
        wgj = sbt("wgj", [128, 1024], BF16)
        WK = sbt("WK", [128, 12 * E], F32)
        PS = es.enter_context(nc.psum_tensor("PS", [128, 4096], F32))
        S = Sched(nc, es)

        def wk(i, n=E, off_=0):
            return WK[:, i * E + off_: i * E + off_ + n]

        def wkr(i):
            return f"wk{i}"

        A3F = AR3[:, :].bitcast(F32)
        WB = 640
        a3names = [f"a3w{i}" for i in range(9)]

        def a3(i, lo=0, hi_=WB):
            return A3F[:, i * WB + lo: i * WB + hi_]

        h_v = AR1[:, 0:KB * E].rearrange("p (k c) -> p k c", c=E)
        yg_v = AR1[:, KB * E: KB * E + KB * NF].rearrange("p (k c) -> p k c", c=NF)
        hp_v = AR1[:, KB * E: KB * E + KB * TP].rearrange("p (k c) -> p k c", c=TP)
        o32_v = AR1[:, 0:2 * KB * NF].bitcast(F32).rearrange("p (k c) -> p k c", c=NF)
        fin_v = AR1[:, 0:NFF * NB].rearrange("p (k c) -> p k c", c=NB)
        mx_v = AR3[:, :].rearrange("p (k c) -> p k c", c=NF)

        def pc(col, n=1):
            return par[:, col:col + n]

        wst = {"issue": 0, "use": 0}

        def w_prefetch():
            if dry:
                return
            while wst["issue"] < len(plan) and wst["issue"] < wst["use"] + NWS:
                i = wst["issue"]
                cid, nkb = plan[i]
                slot = i % NWS
                S.op("pool", lambda h, slot=slot, cid=cid, nkb=nkb: h.dma_start(
                    out=WS[slot][:, 0:nkb * 128], in_=wall[cid, :, 0:nkb * 128]),
                    W=[f"w{slot}"], dma=True)
                wst["issue"] += 1

        def w_get(cid, nkb):
            i = wst["use"]
            if dry:
                plan.append((cid, nkb))
                wst["use"] += 1
                return WS[i % NWS], f"w{i % NWS}"
            assert plan[i] == (cid, nkb), (i, plan[i], cid, nkb)
            if wst["issue"] <= i:
                w_prefetch()
            wst["use"] += 1
            return WS[i % NWS], f"w{i % NWS}"

        ptc = {"n": 0}

        def ptile():
            t = ptc["n"] % NPT
            ptc["n"] += 1
            return t

        def psv(t, lo, hi):
            return PS[:, t * PTS + lo: t * PTS + hi]

        def mm_group(t, lo, hi, terms, R):
            pcs = pieces(t, lo, hi)
            n = len(terms)

            def fn(h):
                ins = None
                for ki, (lt, rf) in enumerate(terms):
                    for (plo, phi, flat) in pcs:
                        ins = h.matmul(PS[:, flat:flat + (phi - plo)], lhsT=lt, rhs=rf(plo, phi),
                                       start=(ki == 0), stop=(ki == n - 1))
                return ins
            S.op("pe", fn, R=R, W=[f"ps{t}"])

        def bcast_sum(src_ap, src_res, n, out_ap, out_res, scale, eps=None, power=None, add_first=True):
            t = ptile()
            pcs = pieces(t, 0, n)
            hi16 = hl16[:, 0:n]
            lo16 = hl16[:, E:E + n]
            S.op("act", lambda h: h.activation(out=hi16, in_=src_ap, func=AF.Copy), R=[src_res], W=["hl16"])
            S.op("dve", lambda h: h.tensor_tensor(out=src_ap, in0=src_ap, in1=hi16, op=ALU.subtract),
                 R=["hl16"], W=[src_res])
            S.op("act", lambda h: h.activation(out=lo16, in_=src_ap, func=AF.Copy), R=[src_res], W=["hl16"])

            def fn(h):
                ins = None
                for (plo, phi, flat) in pcs:
                    h.matmul(PS[:, flat:flat + (phi - plo)], lhsT=ones[:, :], rhs=hi16[:, plo:phi],
                             start=True, stop=False)
                    ins = h.matmul(PS[:, flat:flat + (phi - plo)], lhsT=ones[:, :], rhs=lo16[:, plo:phi],
                                   start=False, stop=True)
                return ins
            S.op("pe", fn, R=["hl16", "ones"], W=[f"ps{t}"])
            if power is None:
                S.op("dve", lambda h: h.tensor_scalar(out=out_ap, in0=psv(t, 0, n), scalar1=scale, scalar2=None,
                                                      op0=ALU.mult), R=[f"ps{t}"], W=[out_res])
            else:
                S.op("act", lambda h: h.activation(out=out_ap, in_=psv(t, 0, n), func=AF.Sqrt, bias=epsb[:, 0:1],
                                                   scale=scale), R=[f"ps{t}", "epsb"], W=[out_res])
                S.op("dve", lambda h: h.reciprocal(out=out_ap, in_=out_ap), R=[out_res], W=[out_res])

        S.op("sp", lambda h: h.dma_start(out=par[:, :], in_=par_d), W=["par"], dma=True)
        S.op("sp", lambda h: h.dma_start(out=msk[:, :], in_=msk_d), W=["msk"], dma=True)
        S.op("sp", lambda h: h.dma_start(out=sbv[:, :], in_=sb_d), W=["sbv"], dma=True)
        S.op("sp", lambda h: h.dma_start(out=shs[:, :], in_=sh_d), W=["shs"], dma=True)
        S.op("sp", lambda h: h.dma_start(out=sfs[:, :], in_=sf_d), W=["sfs"], dma=True)
        S.op("dve", lambda h: h.memset(ones[:, :], 1.0), W=["ones"])
        S.op("dve", lambda h: h.memset(epsb[:, 0:1], EPS), W=["epsb"])
        S.op("dve", lambda h: h.memset(epsb[:, 1:2], 1.0), W=["epsb"])
        S.op("act", lambda h: h.activation(out=der[:, 0:16], in_=pc(P_LAM, 16), func=AF.Exp, scale=-1.0),
             R=["par"], W=["der"])
        S.op("act", lambda h: h.activation(out=der[:, 0:16], in_=der[:, 0:16], func=AF.Ln, bias=1.0, scale=1.0),
             R=["der"], W=["der"])
        S.op("dve", lambda h: h.tensor_scalar(out=der[:, 16:32], in0=der[:, 0:16], scalar1=-8.0, scalar2=None,
                                              op0=ALU.mult), R=["der"], W=["der"])
        S.op("dve", lambda h: h.tensor_scalar(out=der[:, 0:16], in0=der[:, 0:16], scalar1=-4.0, scalar2=None,
                                              op0=ALU.mult), R=["der"], W=["der"])
        S.op("dve", lambda h: h.tensor_scalar(out=der[:, 32:48], in0=pc(P_BR, 16), scalar1=0.5, scalar2=None,
                                              op0=ALU.mult), R=["par", "der"], W=["der"])
        S.op("dve", lambda h: h.tensor_scalar(out=der[:, 48:64], in0=pc(P_BI, 16), scalar1=0.5, scalar2=None,
                                              op0=ALU.mult), R=["par", "der"], W=["der"])
        S.op("dve", lambda h: h.memset(bxst[:, :], 0.0), W=["bxst"])
        S.op("dve", lambda h: h.memset(hraw[:, :], 0.0), W=["hraw"])
        w_prefetch()

        def rmsnorm_in(src, n, hview, hres, lo_res):
            NXS = 6
            for kb in range(KB):
                xs_i = kb % NXS
                sq_i = 6 + kb % 2
                S.op("sp", lambda h, kb=kb, xs_i=xs_i: h.dma_start(out=a3(xs_i)[:, 0:n], in_=src[kb]),
                     W=[a3names[xs_i]], dma=True)
                S.op("act", lambda h, xs_i=xs_i, sq_i=sq_i: h.activation(out=a3(sq_i)[:, 0:n], in_=a3(xs_i)[:, 0:n],
                                                                          func=AF.Square),
                     R=[a3names[xs_i]], W=[a3names[sq_i]])
                if kb == 0:
                    S.op("pool", lambda h, sq_i=sq_i: h.tensor_copy(out=wk(5, n), in_=a3(sq_i)[:, 0:n]),
                         R=[a3names[sq_i]], W=[wkr(5)])
                else:
                    S.op("pool", lambda h, sq_i=sq_i: h.tensor_tensor(out=wk(5, n), in0=wk(5, n),
                                                                      in1=a3(sq_i)[:, 0:n], op=ALU.add),
                         R=[a3names[sq_i]], W=[wkr(5)])
            bcast_sum(wk(5, n), wkr(5), n, wk(6, n), wkr(6), 1.0 / D, EPS, -0.5)
            for kb in range(KB):
                xs_i = kb % NXS
                S.op("sp", lambda h, kb=kb, xs_i=xs_i: h.dma_start(out=a3(xs_i)[:, 0:n], in_=src[kb]),
                     W=[a3names[xs_i]], dma=True)
                S.op("dve", lambda h, kb=kb, xs_i=xs_i: h.scalar_tensor_tensor(
                    out=hview[:, kb, 0:n], in0=a3(xs_i)[:, 0:n], scalar=pc(P_GPRE + kb), in1=wk(6, n),
                    op0=ALU.mult, op1=ALU.mult), R=[a3names[xs_i], wkr(6), "par"], W=[f"{hres}{kb}"])

        def branch_b(j, prefix, hview, hres, hi, rnd):
            if prefix:
                clo, chi = 0, TP
                npr = TP
                nsm = 0
            else:
                clo, chi = 32, E
                npr = 512
                nsm = 2
            ntok = npr + nsm * 16
            bi = j % 2
            gi = j % 4
            for jj in ([0, 1, 2] if j == 0 else ([j + 2] if j + 2 < 16 else [])):
                S.op("pool", lambda h, jj=jj: h.dma_start(out=wgj[:, (jj % 4) * 256:(jj % 4 + 1) * 256], in_=wg_d[jj]),
                     W=[f"wgj{jj % 4}"], dma=True)
            wt, wr = w_get(32 + j, 32)
            tx = ptile()
            mm_group(tx, clo, chi, [(wt[:, kb * 128:(kb + 1) * 128],
                                     (lambda plo, phi, kb=kb: hview[:, kb, plo:phi])) for kb in range(KB)],
                     R=[wr] + [f"{hres}{kb}" for kb in range(KB)])
            w_prefetch()
            if not prefix:
                wt2, wr2 = w_get(48 + j, 32)
                tg = ptile()
                mm_group(tg, FO, E, [(wt2[:, kb * 128:(kb + 1) * 128],
                                      (lambda plo, phi, kb=kb: hview[:, kb, plo:phi])) for kb in range(KB)],
                         R=[wr2] + [f"{hres}{kb}" for kb in range(KB)])
                w_prefetch()
            bx = wk(0 + bi)
            bxr = wkr(0 + bi)
            S.op("act", lambda h: h.activation(out=bx[:, 3:3 + npr], in_=psv(tx, clo, clo + npr), func=AF.Copy),
                 R=[f"ps{tx}"], W=[bxr])
            S.op("dve", lambda h: h.tensor_copy(out=bx[:, 0:3], in_=bxst[:, j * 3:j * 3 + 3]),
                 R=["bxst"], W=[bxr])
            if nsm:
                bxs = bx[:, 3 + npr:3 + npr + 38].rearrange("p (s k) -> p s k", k=19)
                S.op("act", lambda h: h.activation(
                    out=bxs[:, :, 3:19], in_=psv(tx, PE_, E).rearrange("p (s k) -> p s k", k=16), func=AF.Copy),
                    R=[f"ps{tx}"], W=[bxr])
                sbo = hi * 96 + j * 6
                S.op("dve", lambda h: h.tensor_copy(
                    out=bxs[:, :, 0:3], in_=sbv[:, sbo:sbo + 6].rearrange("p (s k) -> p s k", k=3)),
                    R=["sbv"], W=[bxr])
            xb = wk(2 + bi, ntok)
            xbr = wkr(2 + bi)
            wcol = P_WB + j * 4
            S.op("dve", lambda h: h.tensor_scalar(out=xb[:, 0:npr], in0=bx[:, 0:npr], scalar1=pc(wcol),
                                                  scalar2=pc(P_BB + j), op0=ALU.mult, op1=ALU.add),
                 R=[bxr, "par"], W=[xbr])
            for k in range(1, 4):
                S.op("dve", lambda h, k=k: h.scalar_tensor_tensor(
                    out=xb[:, 0:npr], in0=bx[:, k:k + npr], scalar=pc(wcol + k), in1=xb[:, 0:npr],
                    op0=ALU.mult, op1=ALU.add), R=[bxr, "par"], W=[xbr])
            if nsm:
                xbs = xb[:, npr:npr + 32].rearrange("p (s k) -> p s k", k=16)
                S.op("dve", lambda h: h.tensor_scalar(out=xbs, in0=bxs[:, :, 0:16], scalar1=pc(wcol),
                                                      scalar2=pc(P_BB + j), op0=ALU.mult, op1=ALU.add),
                     R=[bxr, "par"], W=[xbr])
                for k in range(1, 4):
                    S.op("dve", lambda h, k=k: h.scalar_tensor_tensor(
                        out=xbs, in0=bxs[:, :, k:k + 16], scalar=pc(wcol + k), in1=xbs,
                        op0=ALU.mult, op1=ALU.add), R=[bxr, "par"], W=[xbr])
            if prefix:
                S.op("dve", lambda h: h.tensor_copy(out=bxst[:, j * 3:j * 3 + 3], in_=bx[:, npr:npr + 3]),
                     R=[bxr], W=["bxst"])
            else:
                S.op("dve", lambda h: h.tensor_copy(out=bxst[:, j * 3:j * 3 + 3], in_=bx[:, npr:npr + 3]),
                     R=[bxr], W=["bxst"])
                ob = hi * 144 + j * 9
                S.op("dve", lambda h: h.tensor_copy(out=ocb[:, ob:ob + 3], in_=bx[:, npr:npr + 3]),
                     R=[bxr], W=["ocb"])
                S.op("dve", lambda h: h.tensor_copy(
                    out=ocb[:, ob + 3:ob + 9].rearrange("p (s k) -> p s k", k=3), in_=bxs[:, :, 16:19]),
                    R=[bxr], W=["ocb"])
            x16 = wk(4)[:, bi * 288:bi * 288 + 272].bitcast(BF16)
            x16r = wkr(4)
            S.op("act", lambda h: h.activation(out=x16[:, 0:ntok], in_=xb, func=AF.Copy), R=[xbr], W=[x16r])
            tr_ = ptile()
            ti_ = ptile()
            for (tt, go) in ((tr_, 0), (ti_, 128)):
                pcs = pieces(tt, 0, ntok)

                def fn(h, pcs=pcs, go=go):
                    ins = None
                    for (plo, phi, flat) in pcs:
                        ins = h.matmul(PS[:, flat:flat + (phi - plo)],
                                       lhsT=wgj[:, gi * 256 + go: gi * 256 + go + 128], rhs=x16[:, plo:phi],
                                       start=True, stop=True)
                    return ins
                S.op("pe", fn, R=[x16r, f"wgj{gi}"], W=[f"ps{tt}"])
            ab, a2b, vb, hbb = wk(7, ntok), wk(8, ntok), wk(9, ntok), wk(10, ntok)
            trb, trbr = a3(bi)[:, 0:ntok], a3names[bi]
            tib, tibr = a3(2 + bi)[:, 0:ntok], a3names[2 + bi]
            gl, glr = a3(4 + bi)[:, 0:NF], a3names[4 + bi]
            S.op("act", lambda h: h.activation(out=trb, in_=psv(tr_, 0, ntok), func=AF.Tanh,
                                               bias=der[:, 32 + j:33 + j], scale=0.5),
                 R=[f"ps{tr_}", "der"], W=[trbr])
            S.op("act", lambda h: h.activation(out=tib, in_=psv(ti_, 0, ntok), func=AF.Tanh,
                                               bias=der[:, 48 + j:49 + j], scale=0.5),
                 R=[f"ps{ti_}", "der"], W=[tibr])
            if not prefix:
                S.op("act", lambda h: h.activation(out=gl, in_=psv(tg, FO, E), func=AF.Gelu_apprx_tanh),
                     R=[f"ps{tg}"], W=[glr])
            yield
            S.op("act", lambda h: h.activation(out=ab, in_=trb, func=AF.Exp, bias=der[:, j:j + 1],
                                               scale=der[:, j:j + 1]), R=[trbr, "der"], W=[wkr(7)])
            S.op("act", lambda h: h.activation(out=a2b, in_=trb, func=AF.Exp, bias=der[:, 16 + j:17 + j],
                                               scale=der[:, 16 + j:17 + j]), R=[trbr, "der"], W=[wkr(8)])
            S.op("dve", lambda h: h.scalar_tensor_tensor(out=vb, in0=tib, scalar=1.0, in1=xb, op0=ALU.add,
                                                         op1=ALU.mult), R=[tibr, xbr], W=[wkr(9)])
            S.op("act", lambda h: h.activation(out=a2b, in_=a2b, func=AF.Sqrt, bias=epsb[:, 1:2], scale=-1.0),
                 R=[wkr(8), "epsb"], W=[wkr(8)])
            S.op("dve", lambda h: h.scalar_tensor_tensor(out=vb, in0=a2b, scalar=0.5, in1=vb, op0=ALU.mult,
                                                         op1=ALU.mult), R=[wkr(8), wkr(9)], W=[wkr(9)])
            if prefix:
                init = hraw[:, j:j + 1]
                initr = "hraw"
            else:
                init = h0b[:, j:j + 1]
                initr = "h0b"
            S.op("dve", lambda h: h.tensor_tensor_scan(out=hbb[:, 0:npr], data0=ab[:, 0:npr], data1=vb[:, 0:npr],
                                                       initial=init, op0=ALU.mult, op1=ALU.add),
                 R=[wkr(7), wkr(9), initr], W=[wkr(10)])
            for s in range(nsm):
                so = hi * 32 + j * 2 + s
                S.op("dve", lambda h, s=s, so=so: h.tensor_tensor_scan(
                    out=hbb[:, npr + 16 * s:npr + 16 * s + 16], data0=ab[:, npr + 16 * s:npr + 16 * s + 16],
                    data1=vb[:, npr + 16 * s:npr + 16 * s + 16], initial=shs[:, so:so + 1],
                    op0=ALU.mult, op1=ALU.add), R=[wkr(7), wkr(9), "shs"], W=[wkr(10)])
            if prefix:
                S.op("dve", lambda h: h.tensor_copy(out=hraw[:, j:j + 1], in_=hbb[:, npr - 1:npr]),
                     R=[wkr(10)], W=["hraw"])
                if rnd == 1:
                    S.op("dve", lambda h: h.tensor_copy(out=hbhl[:, 2 * j:2 * j + 2], in_=hbb[:, npr - 2:npr]),
                         R=[wkr(10)], W=["hbhl"])
                    S.op("dve", lambda h: h.tensor_scalar(out=h0b[:, j:j + 1], in0=hbb[:, npr - 1:npr],
                                                          scalar1=msk[:, 0:1], scalar2=None, op0=ALU.mult),
                         R=[wkr(10), "msk"], W=["h0b"])
            else:
                S.op("dve", lambda h: h.tensor_tensor(out=yg_v[:, 16 + j, 0:2], in0=gl[:, 0:2],
                                                      in1=hbhl[:, 2 * j:2 * j + 2], op=ALU.mult),
                     R=[glr, "hbhl"], W=[f"yg{16 + j}"])
                S.op("dve", lambda h: h.tensor_tensor(out=yg_v[:, 16 + j, 2:NF], in0=gl[:, 2:NF], in1=hbb,
                                                      op=ALU.mult), R=[glr, wkr(10)], W=[f"yg{16 + j}"])
                ol = hi * 48 + j * 3
                S.op("dve", lambda h: h.tensor_copy(out=olr[:, ol:ol + 1], in_=hbb[:, npr - 1:npr]),
                     R=[wkr(10)], W=["olr"])
                S.op("dve", lambda h: h.tensor_copy(out=olr[:, ol + 1:ol + 2], in_=hbb[:, npr + 15:npr + 16]),
                     R=[wkr(10)], W=["olr"])
                S.op("dve", lambda h: h.tensor_copy(out=olr[:, ol + 2:ol + 3], in_=hbb[:, npr + 31:npr + 32]),
                     R=[wkr(10)], W=["olr"])
                S.op("dve", lambda h: h.tensor_copy(out=hbhl[:, 2 * j:2 * j + 2], in_=hbb[:, npr - 2:npr]),
                     R=[wkr(10)], W=["hbhl"])
                S.op("dve", lambda h: h.tensor_copy(out=h0b[:, j:j + 1], in_=hbb[:, npr - 1:npr]),
                     R=[wkr(10)], W=["h0b"])

        def stage(n):
            if STAGE < n:
                raise _Stop()

        def run_bb(prefix, hview, hres_, hi, rnd):
            prev = None
            for j in range(16):
                g = branch_b(j, prefix, hview, hres_, hi, rnd)
                next(g)
                if prev is not None:
                    for _ in prev:
                        pass
                prev = g
            for _ in prev:
                pass

        def one_half(hi):
            stage(2 if hi == 0 else 7)
            inject = [(S.csem[e], S.cnt[e]) for e in ("pe",) if S.cnt[e] > 0]
            for nm in a3names:
                d = S.readers.setdefault(nm, {})
                for (k, v) in inject:
                    d[k] = max(d.get(k, 0), v)
            rmsnorm_in(xm[hi], E, h_v, "h", None)
            hres = [f"h{kb}" for kb in range(KB)]

            sa_flat = A3F[:, 9 * WB:9 * WB + 960]
            sa_all = sa_flat.rearrange("p (j c) -> p j c", c=60)
            S.op("sp", lambda h, hi=hi: h.dma_start(out=sa_flat, in_=sa_d[hi]),
                 W=["a3sa"] + a3names, dma=True)
            NOFF = int(os.environ.get("MK_NOFF", "0"))
            off_taps = list(range(0, NOFF))
            dve_taps = list(range(NOFF, 31))
            NO = 606
            for j in range(16):
                bi = j % 2
                wt, wr = w_get(j, 32)
                tv = ptile()
                mm_group(tv, 0, E, [(wt[:, kb * 128:(kb + 1) * 128],
                                     (lambda plo, phi, kb=kb: h_v[:, kb, plo:phi])) for kb in range(KB)],
                         R=[wr] + hres)
                w_prefetch()
                wt2, wr2 = w_get(16 + j, 32)
                tg = ptile()
                mm_group(tg, 0, E, [(wt2[:, kb * 128:(kb + 1) * 128],
                                     (lambda plo, phi, kb=kb: h_v[:, kb, plo:phi])) for kb in range(KB)],
                         R=[wr2] + hres)
                w_prefetch()
                sg = wk(0 + bi)
                S.op("act", lambda h, sg=sg, tg=tg: h.activation(out=sg, in_=psv(tg, 0, E), func=AF.Sigmoid),
                     R=[f"ps{tg}"], W=[wkr(0 + bi)])
                ua = a3(bi)
                uar = a3names[bi]
                uas = ua[:, 544:636].rearrange("p (s k) -> p s k", k=46)
                S.op("dve", lambda h, ua=ua, sg=sg, tv=tv: h.tensor_tensor(
                    out=ua[:, 0:PE_], in0=psv(tv, 0, PE_), in1=sg[:, 0:PE_], op=ALU.mult),
                    R=[f"ps{tv}", wkr(0 + bi)], W=[uar])
                S.op("dve", lambda h, uas=uas, sg=sg, tv=tv: h.tensor_tensor(
                    out=uas[:, :, 30:46], in0=psv(tv, PE_, E).rearrange("p (s k) -> p s k", k=16),
                    in1=sg[:, PE_:E].rearrange("p (s k) -> p s k", k=16), op=ALU.mult),
                    R=[f"ps{tv}", wkr(0 + bi)], W=[uar])
                S.op("act", lambda h, uas=uas, j=j: h.activation(
                    out=uas[:, :, 0:30], in_=sa_all[:, j, :].rearrange("p (s k) -> p s k", k=30), func=AF.Copy),
                    R=["a3sa"], W=[uar])
                wcol = P_WA + j * 31
                ypi = 4 if bi == 0 else 7
                yP, yPr = a3(ypi), a3names[ypi]
                for n_, k in enumerate(off_taps):
                    if n_ == 0:
                        S.op("act", lambda h, ua=ua, k=k, wcol=wcol, yP=yP: h.activation(
                            out=yP[:, 0:NO], in_=ua[:, k:k + NO], func=AF.Identity, scale=pc(wcol + k)),
                            R=[uar, "par"], W=[yPr])
                    else:
                        tb_i = 5 + n_ % 2
                        S.op("act", lambda h, ua=ua, k=k, wcol=wcol, tb_i=tb_i: h.activation(
                            out=a3(tb_i)[:, 0:NO], in_=ua[:, k:k + NO], func=AF.Identity, scale=pc(wcol + k)),
                            R=[uar, "par"], W=[a3names[tb_i]])
                        S.op("pool", lambda h, yP=yP, tb_i=tb_i: h.tensor_tensor(
                            out=yP[:, 0:NO], in0=yP[:, 0:NO], in1=a3(tb_i)[:, 0:NO], op=ALU.add),
                            R=[a3names[tb_i]], W=[yPr])
                accs = [(a3(2), a3names[2]), (a3(3), a3names[3])]
                for n_, k in enumerate(dve_taps):
                    ya, yar = accs[n_ % 2]
                    if n_ == 0:
                        S.op("dve", lambda h, ya=ya, ua=ua, k=k, wcol=wcol, j=j: h.tensor_scalar(
                            out=ya[:, 0:NO], in0=ua[:, k:k + NO], scalar1=pc(wcol + k), scalar2=pc(P_BA + j),
                            op0=ALU.mult, op1=ALU.add), R=[uar, "par"], W=[yar])
                    elif n_ == 1:
                        S.op("dve", lambda h, ya=ya, ua=ua, k=k, wcol=wcol: h.tensor_scalar(
                            out=ya[:, 0:NO], in0=ua[:, k:k + NO], scalar1=pc(wcol + k), scalar2=None,
                            op0=ALU.mult), R=[uar, "par"], W=[yar])
                    else:
                        S.op("dve", lambda h, ya=ya, ua=ua, k=k, wcol=wcol: h.scalar_tensor_tensor(
                            out=ya[:, 0:NO], in0=ua[:, k:k + NO], scalar=pc(wcol + k), in1=ya[:, 0:NO],
                            op0=ALU.mult, op1=ALU.add), R=[uar, "par"], W=[yar])
                S.op("dve", lambda h, accs=accs: h.tensor_tensor(out=accs[0][0][:, 0:NO], in0=accs[0][0][:, 0:NO],
                                                                 in1=accs[1][0][:, 0:NO], op=ALU.add),
                     R=[accs[1][1]], W=[accs[0][1]])
                S.op("sp", lambda h, ua=ua, j=j, hi=hi: h.dma_start(out=oca_d[hi, j][:, 0:30], in_=ua[:, 514:544]),
                     R=[uar], dma=True)
                S.op("sp", lambda h, uas=uas, j=j, hi=hi: h.dma_start(
                    out=oca_d[hi, j][:, 30:90].rearrange("p (s k) -> p s k", k=30), in_=uas[:, :, 16:46]),
                    R=[uar], dma=True)
                yb = wk(5 + bi, NF)
                ybr = wkr(5 + bi)
                yA = accs[0][0]
                if NOFF > 0:
                    S.op("pool", lambda h, yb=yb, yA=yA, yP=yP: h.tensor_tensor(
                        out=yb[:, 0:514], in0=yA[:, 0:514], in1=yP[:, 0:514], op=ALU.add),
                        R=[accs[0][1], yPr], W=[ybr])
                    S.op("pool", lambda h, yb=yb, yA=yA, yP=yP: h.tensor_tensor(
                        out=yb[:, 514:546].rearrange("p (s k) -> p s k", k=16),
                        in0=yA[:, 544:636].rearrange("p (s k) -> p s k", k=46)[:, :, 0:16],
                        in1=yP[:, 544:636].rearrange("p (s k) -> p s k", k=46)[:, :, 0:16], op=ALU.add),
                        R=[accs[0][1], yPr], W=[ybr])
                else:
                    S.op("act", lambda h, yb=yb, yA=yA: h.activation(out=yb[:, 0:514], in_=yA[:, 0:514],
                                                                     func=AF.Copy), R=[accs[0][1]], W=[ybr])
                    S.op("act", lambda h, yb=yb, yA=yA: h.activation(
                        out=yb[:, 514:546].rearrange("p (s k) -> p s k", k=16),
                        in_=yA[:, 544:636].rearrange("p (s k) -> p s k", k=46)[:, :, 0:16], func=AF.Copy),
                        R=[accs[0][1]], W=[ybr])
                S.op("act", lambda h, yb=yb: h.activation(out=wk(7, NF), in_=yb, func=AF.Square),
                     R=[ybr], W=[wkr(7)])
                if j == 0:
                    S.op("pool", lambda h, yb=yb: h.tensor_copy(out=wk(8, NF), in_=yb), R=[ybr], W=[wkr(8)])
                    S.op("pool", lambda h: h.tensor_copy(out=wk(9, NF), in_=wk(7, NF)), R=[wkr(7)], W=[wkr(9)])
                else:
                    S.op("pool", lambda h, yb=yb: h.tensor_tensor(out=wk(8, NF), in0=wk(8, NF), in1=yb, op=ALU.add),
                         R=[ybr], W=[wkr(8)])
                    S.op("pool", lambda h: h.tensor_tensor(out=wk(9, NF), in0=wk(9, NF), in1=wk(7, NF), op=ALU.add),
                         R=[wkr(7)], W=[wkr(9)])
                S.op("act", lambda h, yb=yb, j=j: h.activation(out=yg_v[:, j, :], in_=yb, func=AF.Copy),
                     R=[ybr], W=[f"yg{j}"])
            bcast_sum(wk(8, NF), wkr(8), NF, wk(10, NF), wkr(10), 1.0 / 2048)
            bcast_sum(wk(9, NF), wkr(9), NF, wk(11, NF), wkr(11), 1.0 / 2048)
            S.op("dve", lambda h: h.tensor_tensor(out=wk(7, NF), in0=wk(10, NF), in1=wk(10, NF), op=ALU.mult),
                 R=[wkr(10)], W=[wkr(7)])
            S.op("dve", lambda h: h.tensor_tensor(out=wk(11, NF), in0=wk(11, NF), in1=wk(7, NF), op=ALU.subtract),
                 R=[wkr(7), wkr(11)], W=[wkr(11)])
            S.op("act", lambda h: h.activation(out=wk(11, NF), in_=wk(11, NF), func=AF.Sqrt, bias=epsb[:, 0:1],
                                               scale=1.0), R=[wkr(11), "epsb"], W=[wkr(11)])
            S.op("dve", lambda h: h.reciprocal(out=wk(11, NF), in_=wk(11, NF)), R=[wkr(11)], W=[wkr(11)])
            S.op("dve", lambda h: h.scalar_tensor_tensor(out=wk(7, NF), in0=wk(10, NF), scalar=-1.0, in1=wk(11, NF),
                                                         op0=ALU.mult, op1=ALU.mult),
                 R=[wkr(10), wkr(11)], W=[wkr(7)])
            for j in range(16):
                tb = wk(j % 2, NF)
                tbr = wkr(j % 2)
                S.op("dve", lambda h, tb=tb, j=j: h.tensor_tensor(out=tb, in0=yg_v[:, j, :], in1=wk(11, NF),
                                                                 op=ALU.mult), R=[f"yg{j}", wkr(11)], W=[tbr])
                S.op("dve", lambda h, tb=tb: h.tensor_tensor(out=tb, in0=tb, in1=wk(7, NF), op=ALU.add),
                     R=[wkr(7)], W=[tbr])
                S.op("act", lambda h, tb=tb, j=j: h.activation(out=yg_v[:, j, :], in_=tb, func=AF.Silu,
                                                               bias=pc(P_LB + j), scale=pc(P_LG + j)),
                     R=[tbr, "par"], W=[f"yg{j}"])
            stage(3)
            run_bb(False, h_v, "h", hi, 0)
            stage(4)
            inject = [(S.csem[e], S.cnt[e]) for e in ("act", "dve", "pool")]
            for kb in range(KB):
                d = S.readers.setdefault(f"mx{kb}", {})
                for (k, v) in inject:
                    d[k] = max(d.get(k, 0), v)
            for j in range(32):
                bi = j % 2
                wab, wabr = w_get(CID_AB + j, 32)
                ta = ptile()
                mm_group(ta, FO, E, [(wab[:, kb * 128:(kb + 1) * 128],
                                      (lambda plo, phi, kb=kb: yg_v[:, kb, plo - FO:phi - FO])) for kb in range(16)],
                         R=[wabr] + [f"yg{kb}" for kb in range(16)])
                tbb = ptile()
                mm_group(tbb, FO, E, [(wab[:, (16 + kb) * 128:(17 + kb) * 128],
                                       (lambda plo, phi, kb=kb: yg_v[:, 16 + kb, plo - FO:phi - FO]))
                                      for kb in range(16)],
                         R=[wabr] + [f"yg{16 + kb}" for kb in range(16)])
                w_prefetch()
                wma, wmar = w_get(64 + j, 32)
                tma = ptile()
                mm_group(tma, FO, E, [(wma[:, kb * 128:(kb + 1) * 128],
                                       (lambda plo, phi, kb=kb: h_v[:, kb, plo:phi])) for kb in range(KB)],
                         R=[wmar] + hres)
                w_prefetch()
                wmb, wmbr = w_get(96 + j, 32)
                tmb = ptile()
                mm_group(tmb, FO, E, [(wmb[:, kb * 128:(kb + 1) * 128],
                                       (lambda plo, phi, kb=kb: h_v[:, kb, plo:phi])) for kb in range(KB)],
                         R=[wmbr] + hres)
                w_prefetch()
                sga, sgb, t1, t2 = wk(0 + bi, NF), wk(2 + bi, NF), wk(4 + bi, NF), wk(6 + bi, NF)
                S.op("act", lambda h, sga=sga, tma=tma: h.activation(out=sga, in_=psv(tma, FO, E), func=AF.Sigmoid),
                     R=[f"ps{tma}"], W=[wkr(0 + bi)])
                S.op("dve", lambda h, t1=t1, sga=sga, ta=ta: h.tensor_tensor(out=t1, in0=psv(ta, FO, E), in1=sga,
                                                                             op=ALU.mult),
                     R=[f"ps{ta}", wkr(0 + bi)], W=[wkr(4 + bi)])
                S.op("act", lambda h, sgb=sgb, tmb=tmb: h.activation(out=sgb, in_=psv(tmb, FO, E), func=AF.Sigmoid),
                     R=[f"ps{tmb}"], W=[wkr(2 + bi)])
                S.op("dve", lambda h, t2=t2, sgb=sgb, tbb=tbb: h.tensor_tensor(out=t2, in0=psv(tbb, FO, E), in1=sgb,
                                                                               op=ALU.mult),
                     R=[f"ps{tbb}", wkr(2 + bi)], W=[wkr(6 + bi)])
                S.op("pool", lambda h, t1=t1, t2=t2, j=j: h.tensor_tensor(out=mx_v[:, j, :], in0=t1, in1=t2,
                                                                         op=ALU.add),
                     R=[wkr(4 + bi), wkr(6 + bi)], W=[f"mx{j}"])
            stage(5)
            for j in range(32):
                bi = j % 2
                wo, wor = w_get(CID_WO + j, 32)
                to = ptile()
                mm_group(to, FO, E, [(wo[:, kb * 128:(kb + 1) * 128],
                                      (lambda plo, phi, kb=kb: mx_v[:, kb, plo - FO:phi - FO])) for kb in range(KB)],
                         R=[wor] + [f"mx{kb}" for kb in range(KB)])
                w_prefetch()
                S.op("act", lambda h, to=to, j=j: h.activation(out=o32_v[:, j, :], in_=psv(to, FO, E), func=AF.Copy),
                     R=[f"ps{to}"], W=[f"o{j}"] + hres + [f"yg{kb}" for kb in range(KB)])
                S.op("act", lambda h, to=to, bi=bi: h.activation(out=wk(0 + bi, NF), in_=psv(to, FO, E),
                                                                 func=AF.Square),
                     R=[f"ps{to}"], W=[wkr(0 + bi)])
                if j == 0:
                    S.op("pool", lambda h, bi=bi: h.tensor_copy(out=wk(2, NF), in_=wk(0 + bi, NF)),
                         R=[wkr(0 + bi)], W=[wkr(2)])
                else:
                    S.op("pool", lambda h, bi=bi: h.tensor_tensor(out=wk(2, NF), in0=wk(2, NF), in1=wk(0 + bi, NF),
                                                                  op=ALU.add), R=[wkr(0 + bi)], W=[wkr(2)])
            bcast_sum(wk(2, NF), wkr(2), NF, wk(3, NF), wkr(3), 1.0 / D, EPS, -0.5)
            def xload(jj):
                xi_ = 4 + jj % 4
                S.op("sp", lambda h, jj=jj, xi_=xi_, hi=hi: h.dma_start(out=wk(xi_, NF), in_=xm[hi, jj][:, FO:E]),
                     W=[wkr(xi_)], dma=True)
            for jj in range(4):
                xload(jj)
            for j in range(32):
                xs_i = 4 + j % 4
                bi = j % 2
                S.op("dve", lambda h, j=j: h.scalar_tensor_tensor(
                    out=o32_v[:, j, :], in0=o32_v[:, j, :], scalar=pc(P_GPM + j), in1=wk(3, NF),
                    op0=ALU.mult, op1=ALU.mult), R=[wkr(3), "par"], W=[f"o{j}"])
                S.op("dve", lambda h, j=j, xs_i=xs_i: h.tensor_tensor(out=o32_v[:, j, :], in0=o32_v[:, j, :],
                                                                      in1=wk(xs_i, NF), op=ALU.add),
                     R=[wkr(xs_i)], W=[f"o{j}"])
                S.op("sp", lambda h, j=j, hi=hi: h.dma_start(out=xmid_d[hi, j], in_=o32_v[:, j, :]),
                     R=[f"o{j}"], W=[f"xmid{j}"], dma=True)
                if j + 4 < 32:
                    xload(j + 4)
                S.op("act", lambda h, j=j, bi=bi: h.activation(out=wk(8 + bi, NF), in_=o32_v[:, j, :],
                                                               func=AF.Square), R=[f"o{j}"], W=[wkr(8 + bi)])
                if j == 0:
                    S.op("pool", lambda h, bi=bi: h.tensor_copy(out=wk(10, NF), in_=wk(8 + bi, NF)),
                         R=[wkr(8 + bi)], W=[wkr(10)])
                else:
                    S.op("pool", lambda h, bi=bi: h.tensor_tensor(out=wk(10, NF), in0=wk(10, NF),
                                                                  in1=wk(8 + bi, NF), op=ALU.add),
                         R=[wkr(8 + bi)], W=[wkr(10)])
            bcast_sum(wk(10, NF), wkr(10), NF, wk(11, NF), wkr(11), 1.0 / D, EPS, -0.5)
            for j in range(32):
                S.op("dve", lambda h, j=j: h.scalar_tensor_tensor(
                    out=mx_v[:, j, :], in0=o32_v[:, j, :], scalar=pc(P_GPF + j), in1=wk(11, NF),
                    op0=ALU.mult, op1=ALU.mult), R=[f"o{j}", wkr(11), "par"], W=[f"mx{kb}" for kb in range(KB)])
                S.op("dve", lambda h, j=j, hi=hi: h.tensor_scalar(
                    out=mx_v[:, j, 0:2], in0=mx_v[:, j, 0:2], scalar1=msk[:, hi:hi + 1], scalar2=None,
                    op0=ALU.mult), R=["msk"], W=[f"mx{kb}" for kb in range(KB)])
            stage(6)
            h2res = [f"mx{kb}" for kb in range(KB)]
            allo = [f"o{j}" for j in range(32)]
            for m in range(NFF):
                bi = m % 2
                cs = []
                for half_, ch in ((0, m), (1, NFF + m)):
                    wu, wur = w_get(CID_UP + ch, 32)
                    tu = ptile()
                    mm_group(tu, FO, E, [(wu[:, kb * 128:(kb + 1) * 128],
                                          (lambda plo, phi, kb=kb: mx_v[:, kb, plo - FO:phi - FO]))
                                         for kb in range(KB)], R=[wur] + h2res)
                    w_prefetch()
                    cb_i = (0 if half_ == 0 else 2) + bi
                    cbuf = wk(cb_i, NB)
                    wcol = P_WF + ch * 3
                    S.op("dve", lambda h, cbuf=cbuf, tu=tu, wcol=wcol, ch=ch: h.tensor_scalar(
                        out=cbuf[:, 0:512], in0=psv(tu, FO, FO + 512), scalar1=pc(wcol), scalar2=pc(P_BF + ch),
                        op0=ALU.mult, op1=ALU.add), R=[f"ps{tu}", "par"], W=[wkr(cb_i)])
                    for k in (1, 2):
                        S.op("dve", lambda h, cbuf=cbuf, tu=tu, wcol=wcol, k=k: h.scalar_tensor_tensor(
                            out=cbuf[:, 0:512], in0=psv(tu, FO + k, FO + k + 512), scalar=pc(wcol + k),
                            in1=cbuf[:, 0:512], op0=ALU.mult, op1=ALU.add), R=[f"ps{tu}", "par"], W=[wkr(cb_i)])
                    spb = wk(6)[:, (half_ * 2 + bi) * 40:(half_ * 2 + bi) * 40 + 36].rearrange("p (s k) -> p s k", k=18)
                    spr = wkr(6)
                    S.op("act", lambda h, spb=spb, tu=tu: h.activation(
                        out=spb[:, :, 2:18], in_=psv(tu, PE_, E).rearrange("p (s k) -> p s k", k=16), func=AF.Copy),
                        R=[f"ps{tu}"], W=[spr])
                    so = (hi * 172 + ch) * 4
                    S.op("act", lambda h, spb=spb, so=so: h.activation(
                        out=spb[:, :, 0:2], in_=sfs[:, so:so + 4].rearrange("p (s k) -> p s k", k=2), func=AF.Copy),
                        R=["sfs"], W=[spr])
                    cbs = cbuf[:, 512:544].rearrange("p (s k) -> p s k", k=16)
                    S.op("dve", lambda h, cbs=cbs, spb=spb, wcol=wcol, ch=ch: h.tensor_scalar(
                        out=cbs, in0=spb[:, :, 0:16], scalar1=pc(wcol), scalar2=pc(P_BF + ch),
                        op0=ALU.mult, op1=ALU.add), R=[spr, "par"], W=[wkr(cb_i)])
                    for k in (1, 2):
                        S.op("dve", lambda h, cbs=cbs, spb=spb, wcol=wcol, k=k: h.scalar_tensor_tensor(
                            out=cbs, in0=spb[:, :, k:k + 16], scalar=pc(wcol + k), in1=cbs,
                            op0=ALU.mult, op1=ALU.add), R=[spr, "par"], W=[wkr(cb_i)])
                    fo = (hi * 172 + ch) * 6
                    S.op("act", lambda h, tu=tu, fo=fo: h.activation(out=off[:, fo:fo + 2], in_=psv(tu, PE_ - 2, PE_),
                                                                     func=AF.Copy), R=[f"ps{tu}"], W=["off"])
                    S.op("act", lambda h, spb=spb, fo=fo: h.activation(
                        out=off[:, fo + 2:fo + 6].rearrange("p (s k) -> p s k", k=2), in_=spb[:, :, 16:18],
                        func=AF.Copy), R=[spr], W=["off"])
                    cs.append((cbuf, cb_i))
                ga = wk(4 + bi, NB)
                S.op("act", lambda h, ga=ga, c0=cs[0][0]: h.activation(out=ga, in_=c0, func=AF.Gelu_apprx_tanh),
                     R=[wkr(cs[0][1])], W=[wkr(4 + bi)])
                S.op("dve", lambda h, ga=ga, c1=cs[1][0], m=m: h.tensor_tensor(out=fin_v[:, m, :], in0=ga, in1=c1,
                                                                              op=ALU.mult),
                     R=[wkr(4 + bi), wkr(cs[1][1])], W=[f"fin{m}"] + (allo + hres if False else allo))
            finres = [f"fin{m}" for m in range(NFF)]
            inj3 = [(S.csem[e], S.cnt[e]) for e in ("pe", "act", "dve", "pool")]
            for nm in a3names:
                d = S.readers.setdefault(nm, {})
                for (k, v) in inj3:
                    d[k] = max(d.get(k, 0), v)

            def fslots(jj):
                s8 = jj % 8
                if s8 < 4:
                    return wk(s8, NB), wkr(s8), wk(8 + s8, NB), wkr(8 + s8)
                return a3(s8 - 4)[:, 0:NB], a3names[s8 - 4], a3(s8)[:, 0:NB], a3names[s8]

            def final_loads(jj):
                fb, fbr, xb_, xbr_ = fslots(jj)
                if jj >= 4 or True:
                    S.op("sp", lambda h, fb=fb, jj=jj, hi=hi: h.dma_start(out=fb, in_=f_d[hi, jj][:, 0:NB]),
                         R=[f"fd{jj}"], W=[fbr], dma=True)
                    S.op("sp", lambda h, xb_=xb_, jj=jj, hi=hi: h.dma_start(out=xb_, in_=xmid_d[hi, jj][:, 2:NF]),
                         R=[f"xmid{jj}"], W=[xbr_], dma=True)

            for j in range(32):
                bi = j % 2
                if 5 <= j <= 8:
                    final_loads(j - 1)
                tf = ptile()
                pcs_f = pieces(tf, FO + 2, E)
                for part, nkb in ((0, 32), (1, 32), (2, 22)):
                    wd, wdr = w_get(CID_DN + 3 * j + part, nkb)

                    def fn(h, wd=wd, part=part, nkb=nkb, pcs_f=pcs_f):
                        ins = None
                        for kb in range(nkb):
                            m = part * 32 + kb
                            for (plo, phi, flat) in pcs_f:
                                ins = h.matmul(PS[:, flat:flat + (phi - plo)], lhsT=wd[:, kb * 128:(kb + 1) * 128],
                                               rhs=fin_v[:, m, plo - FO - 2:phi - FO - 2],
                                               start=(m == 0), stop=(m == NFF - 1))
                        return ins
                    S.op("pe", fn, R=[wdr] + finres[part * 32:part * 32 + nkb], W=[f"ps{tf}"])
                    w_prefetch()
                fb = wk(8 + bi, NB)
                S.op("act", lambda h, fb=fb, tf=tf: h.activation(out=fb, in_=psv(tf, FO + 2, E), func=AF.Copy),
                     R=[f"ps{tf}"], W=[wkr(8 + bi)])
                S.op("sp", lambda h, fb=fb, j=j, hi=hi: h.dma_start(out=f_d[hi, j][:, 0:NB], in_=fb),
                     R=[wkr(8 + bi)], W=[f"fd{j}"], dma=True)
                S.op("act", lambda h, tf=tf: h.activation(out=wk(10, NB), in_=psv(tf, FO + 2, E), func=AF.Square),
                     R=[f"ps{tf}"], W=[wkr(10)])
                if j == 0:
                    S.op("pool", lambda h: h.tensor_copy(out=wk(11, NB), in_=wk(10, NB)), R=[wkr(10)], W=[wkr(11)])
                else:
                    S.op("pool", lambda h: h.tensor_tensor(out=wk(11, NB), in0=wk(11, NB), in1=wk(10, NB), op=ALU.add),
                         R=[wkr(10)], W=[wkr(11)])
            bcast_sum(wk(11, NB), wkr(11), NB, wk(7, NB), wkr(7), 1.0 / D, EPS, -0.5)
            for j in range(32):
                bi = j % 2
                fb, fbr, xb_, xbr_ = fslots(j)
                if j == 0:
                    for jj in range(4):
                        final_loads(jj)
                S.op("dve", lambda h, fb=fb, j=j: h.scalar_tensor_tensor(
                    out=fb, in0=fb, scalar=pc(P_GPO + j), in1=wk(7, NB), op0=ALU.mult, op1=ALU.mult),
                    R=[wkr(7), "par"], W=[fbr])
                S.op("dve", lambda h, fb=fb, xb_=xb_: h.tensor_tensor(out=fb, in0=fb, in1=xb_, op=ALU.add),
                     R=[xbr_], W=[fbr])
                S.op("sp", lambda h, fb=fb, j=j, hi=hi: h.dma_start(out=y_d[hi, j], in_=fb),
                     R=[fbr], dma=True)
                if j >= 1 and j + 7 < 32:
                    final_loads(j + 7)
            for kb in range(KB):
                S.readers.setdefault(f"h{kb}", {})
                S.readers.setdefault(f"yg{kb}", {})
            lastpe = (S.csem["pe"], S.cnt["pe"])
            lastdve = (S.csem["dve"], S.cnt["dve"])
            lastact = (S.csem["act"], S.cnt["act"])
            for kb in range(KB):
                for nm in (f"h{kb}", f"yg{kb}"):
                    d = S.readers.setdefault(nm, {})
                    for (k, v) in (lastpe, lastdve, lastact):
                        d[k] = max(d.get(k, 0), v)

        try:
            stage(1)
            sub = int(os.environ.get("MK_SUB", "99"))
            for rnd in range(2):
                rmsnorm_in(xp[rnd], TP, hp_v, "yg", None)
                if sub == 1:
                    raise _Stop()
                run_bb(True, hp_v, "yg", 0, rnd)
            for hi in range(2):
                one_half(hi)
        except _Stop:
            pass

        S.op("sp", lambda h: h.dma_start(out=ocb_d, in_=ocb[:, :]), R=["ocb"], dma=True)
        S.op("sp", lambda h: h.dma_start(out=olr_d, in_=olr[:, :]), R=["olr"], dma=True)
        S.op("sp", lambda h: h.dma_start(out=off_d, in_=off[:, :]), R=["off"], dma=True)
        S.final_wait("sp")

        assert dry or wst["use"] == len(plan), (wst, len(plan))

        with nc.Block() as block:
            @block.tensor
            def _(t):
                for f in S.streams["pe"]:
                    f(t)

            @block.scalar
            def _(a):
                for f in S.streams["act"]:
                    f(a)

            @block.vector
            def _(v):
                for f in S.streams["dve"]:
                    f(v)

            @block.gpsimd
            def _(g):
                for f in S.streams["pool"]:
                    f(g)

            @block.sync
            def _(s):
                for f in S.streams["sp"]:
                    f(s)
    return nc


def _tile_w(W):
    K, N = W.shape
    kb, nb = K // 128, N // 128
    return np.ascontiguousarray(W.reshape(kb, 128, nb, 128).transpose(2, 1, 0, 3)).reshape(nb, 128, kb * 128)


def _fm(v, nblk):
    return np.ascontiguousarray(v.reshape(nblk, 128).T)


def _tok_fm(a):
    T, C = a.shape
    return np.ascontiguousarray(a.T).reshape(C // 128, 128, T)


_NC_CACHE = {}


def kernel(x_prompt, x_sample, state_conv_a, state_conv_b, state_lru, state_ffn,
           g_pre_mix, w_in, w_dw_a, b_dw_a, ln_a_g, ln_a_b, w_a_out,
           w_dw_b, b_dw_b, w_rg_r, b_rg_r, w_rg_i, b_rg_i, lru_lambda, w_b_out,
           w_o, g_post_mix, g_pre_ffn, w_up, w_dw_f, b_dw_f, w_down, g_post_ffn):
    f = np.float32
    x_prompt = np.asarray(x_prompt, f)
    x_sample = np.asarray(x_sample, f)
    wall = np.zeros((NCH, 128, 4096), f)
    wall[0:128] = _tile_w(np.asarray(w_in[0], f))
    wall[CID_AB:CID_AB + 32, :, 0:2048] = _tile_w(np.asarray(w_a_out[0], f))
    wall[CID_AB:CID_AB + 32, :, 2048:4096] = _tile_w(np.asarray(w_b_out[0], f))
    wall[CID_WO:CID_WO + 32] = _tile_w(np.asarray(w_o[0], f))
    wall[CID_UP:CID_UP + 172] = _tile_w(np.asarray(w_up[0], f))
    wd = _tile_w(np.asarray(w_down[0], f))
    dn = wall[CID_DN:CID_DN + 96].reshape(32, 3, 128, 4096)
    dn[:, 0] = wd[:, :, 0:4096]
    dn[:, 1] = wd[:, :, 4096:8192]
    dn[:, 2, :, 0:2816] = wd[:, :, 8192:11008]
    wg = np.concatenate([np.asarray(w_rg_r[0], f), np.asarray(w_rg_i[0], f)], axis=2)
    par = np.zeros((128, NPAR), f)
    par[:, P_GPRE:P_GPRE + 32] = _fm(np.asarray(g_pre_mix[0], f), 32)
    par[:, P_WA:P_WA + 496] = np.asarray(w_dw_a[0], f).reshape(31, 16, 128).transpose(2, 1, 0).reshape(128, 496)
    par[:, P_BA:P_BA + 16] = _fm(np.asarray(b_dw_a[0], f), 16)
    par[:, P_LG:P_LG + 16] = _fm(np.asarray(ln_a_g[0], f), 16)
    par[:, P_LB:P_LB + 16] = _fm(np.asarray(ln_a_b[0], f), 16)
    par[:, P_WB:P_WB + 64] = np.asarray(w_dw_b[0], f).reshape(4, 16, 128).transpose(2, 1, 0).reshape(128, 64)
    par[:, P_BB:P_BB + 16] = _fm(np.asarray(b_dw_b[0], f), 16)
    par[:, P_BR:P_BR + 16] = _fm(np.asarray(b_rg_r[0], f), 16)
    par[:, P_BI:P_BI + 16] = _fm(np.asarray(b_rg_i[0], f), 16)
    par[:, P_LAM:P_LAM + 16] = _fm(np.asarray(lru_lambda[0], f), 16)
    par[:, P_GPM:P_GPM + 32] = _fm(np.asarray(g_post_mix[0], f), 32)
    par[:, P_GPF:P_GPF + 32] = _fm(np.asarray(g_pre_ffn[0], f), 32)
    par[:, P_WF:P_WF + 516] = np.asarray(w_dw_f[0], f).reshape(3, 172, 128).transpose(2, 1, 0).reshape(128, 516)
    par[:, P_BF:P_BF + 172] = _fm(np.asarray(b_dw_f[0], f), 172)
    par[:, P_GPO:P_GPO + 32] = _fm(np.asarray(g_post_ffn[0], f), 32)

    sca = np.asarray(state_conv_a[0], f)
    scb = np.asarray(state_conv_b[0], f)
    slr = np.asarray(state_lru[0], f)
    sff = np.asarray(state_ffn[0], f)
    in_maps = []
    for c in range(NCORES):
        s, hf = c // 2, c % 2
        xm = np.zeros((2, KB, 128, E), f)
        xpre = np.zeros((2, KB, 128, TP), f)
        for hi in range(2):
            p0 = hf * 1024 + hi * 512
            if p0 > 0:
                xm[hi, :, :, 0:32] = _tok_fm(x_prompt[s, p0 - 32:p0])
            xm[hi, :, :, 32:544] = _tok_fm(x_prompt[s, p0:p0 + 512])
            q0 = 4 * c + 2 * hi
            xm[hi, :, :, 544:576] = _tok_fm(x_sample[q0:q0 + 2].reshape(32, D))
        if hf == 1:
            for r in range(2):
                xpre[r] = _tok_fm(x_prompt[s, r * 512:(r + 1) * 512])
        msk = np.zeros((128, 2), f)
        msk[:, 0] = float(hf)
        msk[:, 1] = 1.0
        sa = np.zeros((2, 128, 960), f)
        sbv = np.zeros((128, 2, 16, 2, 3), f)
        sh = np.zeros((128, 2, 16, 2), f)
        sf = np.zeros((128, 2, 172, 2, 2), f)
        for hi in range(2):
            q0 = 4 * c + 2 * hi
            sa[hi] = sca[q0:q0 + 2].reshape(2, 30, 16, 128).transpose(3, 2, 0, 1).reshape(128, 960)
            sbv[:, hi] = scb[q0:q0 + 2].reshape(2, 3, 16, 128).transpose(3, 2, 0, 1)
            sh[:, hi] = slr[q0:q0 + 2].reshape(2, 16, 128).transpose(2, 1, 0)
            sf[:, hi] = sff[q0:q0 + 2].reshape(2, 2, 172, 128).transpose(3, 2, 0, 1)
        in_maps.append({"xm": xm, "xp": xpre, "par": par, "msk": msk, "wg": wg, "wall": wall,
                        "sa": sa, "sbv": sbv.reshape(128, 192), "sh": sh.reshape(128, 64),
                        "sf": sf.reshape(128, 2 * 172 * 4)})
    if "nc" not in _NC_CACHE:
        _NC_CACHE["nc"] = build_program()
    nc, plan = _NC_CACHE["nc"]
    nch = max(c for c, _ in plan) + 1 if plan else 1
    if nch < NCH:
        wall = np.ascontiguousarray(wall[:nch])
        for m_ in in_maps:
            m_["wall"] = wall
    res = run_bass_kernel_spmd(nc, in_maps, core_ids=list(range(NCORES)))
    R = res.results
    yp = np.zeros((4, 2048, D), f)
    ys = np.zeros((32, 16, D), f)
    pa = np.zeros((1, 4, 30, 2048), f)
    pb = np.zeros((1, 4, 3, 2048), f)
    ph = np.zeros((1, 4, 2048), f)
    pf = np.zeros((1, 4, 2, 22016), f)
    sa_o = np.zeros((1, 32, 30, 2048), f)
    sb_o = np.zeros((1, 32, 3, 2048), f)
    sh_o = np.zeros((1, 32, 2048), f)
    sf_o = np.zeros((1, 32, 2, 22016), f)
    for c in range(NCORES):
        s, hf = c // 2, c % 2
        r = R[c]
        y = np.asarray(r["y"]).reshape(2, D, NB)
        oca = np.asarray(r["oca"]).reshape(2, 2048, 3, 30)
        ocb = np.asarray(r["ocb"]).reshape(128, 2, 16, 3, 3)
        olr = np.asarray(r["olr"]).reshape(128, 2, 16, 3)
        off = np.asarray(r["off"]).reshape(128, 2, 172, 3, 2)
        for hi in range(2):
            p0 = hf * 1024 + hi * 512
            yp[s, p0:p0 + 512] = y[hi, :, 0:512].T
            q0 = 4 * c + 2 * hi
            ys[q0:q0 + 2] = y[hi, :, 512:544].T.reshape(2, 16, D)
            ocb_h = ocb[:, hi].transpose(2, 3, 1, 0).reshape(3, 3, 2048)
            olr_h = olr[:, hi].transpose(2, 1, 0).reshape(3, 2048)
            off_h = off[:, hi].transpose(2, 3, 1, 0).reshape(3, 2, 22016)
            oca_h = oca[hi].transpose(1, 2, 0)
            for q in range(2):
                sa_o[0, q0 + q] = oca_h[1 + q]
                sb_o[0, q0 + q] = ocb_h[1 + q]
                sh_o[0, q0 + q] = olr_h[1 + q]
                sf_o[0, q0 + q] = off_h[1 + q]
            if hf == 1 and hi == 1:
                pa[0, s] = oca_h[0]
                pb[0, s] = ocb_h[0]
                ph[0, s] = olr_h[0]
                pf[0, s] = off_h[0]
    return (yp, ys, pa, pb, ph, pf, sa_o, sb_o, sh_o, sf_o)
```

```python
import os
from contextlib import ExitStack
import numpy as np
import concourse.bass as bass
import concourse.mybir as mybir
from concourse.bass_utils import run_bass_kernel_spmd

F32 = mybir.dt.float32
BF16 = mybir.dt.bfloat16
AF = mybir.ActivationFunctionType
ALU = mybir.AluOpType

NCORES = 8
D = 4096
KB = 32
E = 576
FO = 30
NF = E - FO
PE_ = 544
NB = 544
TP = 512
NFF = 86
EPS = 1e-6

P_GPRE = 0
P_WA = 32
P_BA = 528
P_LG = 544
P_LB = 560
P_WB = 576
P_BB = 640
P_BR = 656
P_BI = 672
P_LAM = 688
P_GPM = 704
P_GPF = 736
P_WF = 768
P_BF = 1284
P_GPO = 1456
NPAR = 1488

CID_AB = 128
CID_WO = 160
CID_UP = 192
CID_DN = 364
NCH = 460

SAME_SYNC = os.environ.get("MK_SAME_SYNC", "1") == "1"
NWS = 3
SAME_LAG = int(os.environ.get("MK_SAME_LAG", "1000000"))
NPT = 4
PTS = 1024


class Sched:
    def __init__(self, nc, es, kdma=8):
        self.nc = nc
        self.streams = {e: [] for e in ("pe", "act", "dve", "pool", "sp")}
        self.sems = []
        self.csem = {}
        self.cnt = {}
        for e in ("pe", "act", "dve", "pool"):
            self.csem[e] = self._newsem(es, "c_" + e)
            self.cnt[e] = 0
        self.dq = {}
        for e in ("sp", "pool"):
            self.dq[e] = {"keys": [self._newsem(es, f"d_{e}{i}") for i in range(kdma)], "n": 0}
        self.kdma = kdma
        self.seen = {e: {} for e in self.streams}
        self.last_w = {}
        self.readers = {}

    def _newsem(self, es, name):
        self.sems.append(es.enter_context(self.nc.semaphore(name)))
        return len(self.sems) - 1

    def _waits(self, eng, deps):
        need = {}
        for (k, v) in deps:
            if v > need.get(k, 0):
                need[k] = v
        for k, v in need.items():
            if eng in self.csem and k == self.csem[eng]:
                if (not SAME_SYNC) or eng == "pe":
                    continue
                if self.cnt[eng] - v >= SAME_LAG:
                    continue
            if self.seen[eng].get(k, 0) >= v:
                continue
            self.seen[eng][k] = v
            sem = self.sems[k]
            self.streams[eng].append(lambda h, sem=sem, v=v: h.wait_ge(sem, v))

    def op(self, eng, fn, R=(), W=(), dma=False):
        W = list(W) + [r for r in R if r.startswith("ps")]
        R = [r for r in R if not r.startswith("ps")]
        deps = []
        for r in R:
            t = self.last_w.get(r)
            if t:
                deps.append(t)
        for w in W:
            t = self.last_w.get(w)
            if t:
                deps.append(t)
            deps.extend(self.readers.get(w, {}).items())
        if dma:
            q = self.dq[eng]
            i = q["n"]
            q["n"] += 1
            k = q["keys"][i % self.kdma]
            val = 16 * (i // self.kdma + 1)
            if val > 16:
                deps.append((k, val - 16))
            tok = (k, val)
            inc = 16
        else:
            self.cnt[eng] += 1
            tok = (self.csem[eng], self.cnt[eng])
            inc = 1
        self._waits(eng, deps)
        sem = self.sems[tok[0]]
        self.streams[eng].append(lambda h, fn=fn, sem=sem, inc=inc: fn(h).then_inc(sem, inc))
        if (not dma):
            pass
        for r in R:
            d = self.readers.setdefault(r, {})
            if tok[1] > d.get(tok[0], 0):
                d[tok[0]] = tok[1]
        for w in W:
            self.last_w[w] = tok
            self.readers[w] = {}
        return tok

    def final_wait(self, eng):
        deps = []
        for e, k in self.csem.items():
            if self.cnt[e] > 0:
                deps.append((k, self.cnt[e]))
        for e, q in self.dq.items():
            for j, k in enumerate(q["keys"]):
                n = q["n"]
                cntj = (n - j + self.kdma - 1) // self.kdma if n > j else 0
                if cntj > 0:
                    deps.append((k, 16 * cntj))
        save = SAME_SYNC
        need = {}
        for (k, v) in deps:
            need[k] = max(need.get(k, 0), v)
        for k, v in need.items():
            if self.seen[eng].get(k, 0) >= v:
                continue
            self.seen[eng][k] = v
            sem = self.sems[k]
            self.streams[eng].append(lambda h, sem=sem, v=v: h.wait_ge(sem, v))


def pieces(tile, lo, hi):
    out = []
    base = tile * PTS
    c = lo
    while c < hi:
        flat = base + c
        nxt = min(hi, c + (512 - flat % 512))
        out.append((c, nxt, flat))
        c = nxt
    return out


def gen_plan():
    plan = []
    for r in range(2):
        for j in range(16):
            plan.append((32 + j, 32))
    for hi in range(2):
        for j in range(16):
            plan.append((j, 32))
            plan.append((16 + j, 32))
        for j in range(16):
            plan.append((32 + j, 32))
            plan.append((48 + j, 32))
        for j in range(32):
            plan.append((CID_AB + j, 32))
            plan.append((64 + j, 32))
            plan.append((96 + j, 32))
        for j in range(32):
            plan.append((CID_WO + j, 32))
        for m in range(NFF):
            plan.append((CID_UP + m, 32))
            plan.append((CID_UP + NFF + m, 32))
        for j in range(32):
            plan.append((CID_DN + 3 * j, 32))
            plan.append((CID_DN + 3 * j + 1, 32))
            plan.append((CID_DN + 3 * j + 2, 22))
    return plan


class _Stop(Exception):
    pass


STAGE = int(os.environ.get("MK_STAGE", "99"))


def build_program(plan=None):
    if plan is None:
        req = []
        build_program(plan=req)
        plan_final = [tuple(x) for x in req if x[0] != "end"]
        return build_program(plan=plan_final + [("end",)]), plan_final
    dry = not (len(plan) > 0 and plan[-1] == ("end",))
    if not dry:
        plan = plan[:-1]
    nch = NCH if dry else (max(c for c, _ in plan) + 1 if plan else 1)
    nc = bass.Bass("TRN2", target_bir_lowering=False)
    dt = nc.dram_tensor
    xm = dt("xm", [2, KB, 128, E], F32, kind="ExternalInput").ap()
    xp = dt("xp", [2, KB, 128, TP], F32, kind="ExternalInput").ap()
    par_d = dt("par", [128, NPAR], F32, kind="ExternalInput").ap()
    msk_d = dt("msk", [128, 2], F32, kind="ExternalInput").ap()
    wg_d = dt("wg", [16, 128, 256], F32, kind="ExternalInput").ap()
    wall = dt("wall", [nch, 128, 4096], F32, kind="ExternalInput").ap()
    sa_d = dt("sa", [2, 128, 960], F32, kind="ExternalInput").ap()
    sb_d = dt("sbv", [128, 192], F32, kind="ExternalInput").ap()
    sh_d = dt("sh", [128, 64], F32, kind="ExternalInput").ap()
    sf_d = dt("sf", [128, 2 * 172 * 4], F32, kind="ExternalInput").ap()
    y_d = dt("y", [2, KB, 128, NB], F32, kind="ExternalOutput").ap()
    oca_d = dt("oca", [2, 16, 128, 90], F32, kind="ExternalOutput").ap()
    ocb_d = dt("ocb", [128, 288], F32, kind="ExternalOutput").ap()
    olr_d = dt("olr", [128, 96], F32, kind="ExternalOutput").ap()
    off_d = dt("off", [128, 2 * 172 * 6], F32, kind="ExternalOutput").ap()
    xmid_d = dt("xmid_s", [2, KB, 128, NF], F32).ap()
    f_d = dt("f_s", [2, KB, 128, NF], F32).ap()

    es = ExitStack()
    with es:
        def sbt(name, shape, dtype):
            return es.enter_context(nc.sbuf_tensor(name, shape, dtype))

        AR1 = sbt("AR1", [128, 48128], BF16)
        AR3 = sbt("AR3", [128, KB * NF], BF16)
        WS = [sbt(f"ws{i}", [128, 4096], BF16) for i in range(NWS)]
        par = sbt("par_s", [128, NPAR], F32)
        der = sbt("der", [128, 64], F32)
        msk = sbt("msk_s", [128, 2], F32)
        ones = sbt("ones", [128, 128], BF16)
        hl16 = sbt("hl16", [128, 2 * E], BF16)
        epsb = sbt("epsb", [128, 2], F32)
        sbv = sbt("sbv_s", [128, 192], F32)
        shs = sbt("sh_s", [128, 64], F32)
        sfs = sbt("sf_s", [128, 2 * 172 * 4], F32)
        ocb = sbt("ocb_s", [128, 288], F32)
        olr = sbt("olr_s", [128, 96], F32)
        off = sbt("off_s", [128, 2 * 172 * 6], F32)
        bxst = sbt("bxst", [128, 48], F32)
        hbhl = sbt("hbhl", [128, 32], F32)
        h0b = sbt("h0b", [128, 16], F32)
        hraw = sbt("hraw", [128, 16], F32)
        wgj = sbt("wgj", [128, 1024], BF16)
        WK = sbt("WK", [128, 12 * E], F32)
        PS = es.enter_context(nc.psum_tensor("PS", [128, 4096], F32))
        S = Sched(nc, es)

        def wk(i, n=E, off_=0):
            return WK[:, i * E + off_: i * E + off_ + n]

        def wkr(i):
            return f"wk{i}"

        A3F = AR3[:, :].bitcast(F32)
        WB = 640
        a3names = [f"a3w{i}" for i in range(9)]

        def a3(i, lo=0, hi_=WB):
            return A3F[:, i * WB + lo: i * WB + hi_]

        h_v = AR1[:, 0:KB * E].rearrange("p (k c) -> p k c", c=E)
        yg_v = AR1[:, KB * E: KB * E + KB * NF].rearrange("p (k c) -> p k c", c=NF)
        hp_v = AR1[:, KB * E: KB * E + KB * TP].rearrange("p (k c) -> p k c", c=TP)
        o32_v = AR1[:, 0:2 * KB * NF].bitcast(F32).rearrange("p (k c) -> p k c", c=NF)
        fin_v = AR1[:, 0:NFF * NB].rearrange("p (k c) -> p k c", c=NB)
        mx_v = AR3[:, :].rearrange("p (k c) -> p k c", c=NF)

        def pc(col, n=1):
            return par[:, col:col + n]

        wst = {"issue": 0, "use": 0}

        def w_prefetch():
            if dry:
                return
            while wst["issue"] < len(plan) and wst["issue"] < wst["use"] + NWS:
                i = wst["issue"]
                cid, nkb = plan[i]
                slot = i % NWS
                S.op("pool", lambda h, slot=slot, cid=cid, nkb=nkb: h.dma_start(
                    out=WS[slot][:, 0:nkb * 128], in_=wall[cid, :, 0:nkb * 128]),
                    W=[f"w{slot}"], dma=True)
                wst["issue"] += 1

        def w_get(cid, nkb):
            i = wst["use"]
            if dry:
                plan.append((cid, nkb))
                wst["use"] += 1
                return WS[i % NWS], f"w{i % NWS}"
            assert plan[i] == (cid, nkb), (i, plan[i], cid, nkb)
            if wst["issue"] <= i:
                w_prefetch()
            wst["use"] += 1
            return WS[i % NWS], f"w{i % NWS}"

        ptc = {"n": 0}

        def ptile():
            t = ptc["n"] % NPT
            ptc["n"] += 1
            return t

        def psv(t, lo, hi):
            return PS[:, t * PTS + lo: t * PTS + hi]

        def mm_group(t, lo, hi, terms, R):
            pcs = pieces(t, lo, hi)
            n = len(terms)

            def fn(h):
                ins = None
                for ki, (lt, rf) in enumerate(terms):
                    for (plo, phi, flat) in pcs:
                        ins = h.matmul(PS[:, flat:flat + (phi - plo)], lhsT=lt, rhs=rf(plo, phi),
                                       start=(ki == 0), stop=(ki == n - 1))
                return ins
            S.op("pe", fn, R=R, W=[f"ps{t}"])

        def bcast_sum(src_ap, src_res, n, out_ap, out_res, scale, eps=None, power=None, add_first=True):
            t = ptile()
            pcs = pieces(t, 0, n)
            hi16 = hl16[:, 0:n]
            lo16 = hl16[:, E:E + n]
            S.op("act", lambda h: h.activation(out=hi16, in_=src_ap, func=AF.Copy), R=[src_res], W=["hl16a", "hl16b"])
            S.op("dve", lambda h: h.tensor_tensor(out=src_ap, in0=src_ap, in1=hi16, op=ALU.subtract),
                 R=["hl16a", "hl16b"], W=[src_res])
            S.op("act", lambda h: h.activation(out=lo16, in_=src_ap, func=AF.Copy), R=[src_res], W=["hl16a", "hl16b"])

            def fn(h):
                ins = None
                for (plo, phi, flat) in pcs:
                    h.matmul(PS[:, flat:flat + (phi - plo)], lhsT=ones[:, :], rhs=hi16[:, plo:phi],
                             start=True, stop=False)
                    ins = h.matmul(PS[:, flat:flat + (phi - plo)], lhsT=ones[:, :], rhs=lo16[:, plo:phi],
                                   start=False, stop=True)
                return ins
            S.op("pe", fn, R=["hl16a", "hl16b", "ones"], W=[f"ps{t}"])
            if power is None:
                S.op("dve", lambda h: h.tensor_scalar(out=out_ap, in0=psv(t, 0, n), scalar1=scale, scalar2=None,
                                                      op0=ALU.mult), R=[f"ps{t}"], W=[out_res])
            else:
                S.op("act", lambda h: h.activation(out=out_ap, in_=psv(t, 0, n), func=AF.Sqrt, bias=epsb[:, 0:1],
                                                   scale=scale), R=[f"ps{t}", "epsb"], W=[out_res])
                S.op("dve", lambda h: h.reciprocal(out=out_ap, in_=out_ap), R=[out_res], W=[out_res])

        S.op("sp", lambda h: h.dma_start(out=par[:, :], in_=par_d), W=["par"], dma=True)
        S.op("sp", lambda h: h.dma_start(out=msk[:, :], in_=msk_d), W=["msk"], dma=True)
        S.op("sp", lambda h: h.dma_start(out=sbv[:, :], in_=sb_d), W=["sbv"], dma=True)
        S.op("sp", lambda h: h.dma_start(out=shs[:, :], in_=sh_d), W=["shs"], dma=True)
        S.op("sp", lambda h: h.dma_start(out=sfs[:, :], in_=sf_d), W=["sfs"], dma=True)
        S.op("dve", lambda h: h.memset(ones[:, :], 1.0), W=["ones"])
        S.op("dve", lambda h: h.memset(epsb[:, 0:1], EPS), W=["epsb"])
        S.op("dve", lambda h: h.memset(epsb[:, 1:2], 1.0), W=["epsb"])
        S.op("act", lambda h: h.activation(out=der[:, 0:16], in_=pc(P_LAM, 16), func=AF.Exp, scale=-1.0),
             R=["par"], W=["der"])
        S.op("act", lambda h: h.activation(out=der[:, 0:16], in_=der[:, 0:16], func=AF.Ln, bias=1.0, scale=1.0),
             R=["der"], W=["der"])
        S.op("dve", lambda h: h.tensor_scalar(out=der[:, 16:32], in0=der[:, 0:16], scalar1=-8.0, scalar2=None,
                                              op0=ALU.mult), R=["der"], W=["der"])
        S.op("dve", lambda h: h.tensor_scalar(out=der[:, 0:16], in0=der[:, 0:16], scalar1=-4.0, scalar2=None,
                                              op0=ALU.mult), R=["der"], W=["der"])
        S.op("dve", lambda h: h.tensor_scalar(out=der[:, 32:48], in0=pc(P_BR, 16), scalar1=0.5, scalar2=None,
                                              op0=ALU.mult), R=["par", "der"], W=["der"])
        S.op("dve", lambda h: h.tensor_scalar(out=der[:, 48:64], in0=pc(P_BI, 16), scalar1=0.5, scalar2=None,
                                              op0=ALU.mult), R=["par", "der"], W=["der"])
        S.op("dve", lambda h: h.memset(bxst[:, :], 0.0), W=["bxst"])
        S.op("dve", lambda h: h.memset(hraw[:, :], 0.0), W=["hraw"])
        w_prefetch()

        def rmsnorm_in(src, n, hview, hres, lo_res):
            NXS = 8
            tq = ptile()
            pcs_q = pieces(tq, 0, n)
            for kb in range(KB):
                xs_i = kb % NXS
                sl = kb % 2
                sq16 = hl16[:, sl * E: sl * E + n]
                sqr = "hl16a" if sl == 0 else "hl16b"
                S.op("sp", lambda h, kb=kb, xs_i=xs_i: h.dma_start(out=a3(xs_i)[:, 0:n], in_=src[kb]),
                     W=[a3names[xs_i]], dma=True)
                S.op("act", lambda h, xs_i=xs_i, sq16=sq16: h.activation(out=sq16, in_=a3(xs_i)[:, 0:n],
                                                                          func=AF.Square),
                     R=[a3names[xs_i]], W=[sqr])

                def fnq(h, sq16=sq16, kb=kb):
                    ins = None
                    for (plo, phi, flat) in pcs_q:
                        ins = h.matmul(PS[:, flat:flat + (phi - plo)], lhsT=ones[:, :], rhs=sq16[:, plo:phi],
                                       start=(kb == 0), stop=(kb == KB - 1))
                    return ins
                S.op("pe", fnq, R=[sqr, "ones"], W=[f"ps{tq}"])
            S.op("act", lambda h: h.activation(out=wk(6, n), in_=psv(tq, 0, n), func=AF.Sqrt, bias=epsb[:, 0:1],
                                               scale=1.0 / D), R=[f"ps{tq}", "epsb"], W=[wkr(6)])
            S.op("dve", lambda h: h.reciprocal(out=wk(6, n), in_=wk(6, n)), R=[wkr(6)], W=[wkr(6)])
            for kb in range(KB):
                xs_i = kb % NXS
                S.op("sp", lambda h, kb=kb, xs_i=xs_i: h.dma_start(out=a3(xs_i)[:, 0:n], in_=src[kb]),
                     W=[a3names[xs_i]], dma=True)
                S.op("dve", lambda h, kb=kb, xs_i=xs_i: h.scalar_tensor_tensor(
                    out=hview[:, kb, 0:n], in0=a3(xs_i)[:, 0:n], scalar=pc(P_GPRE + kb), in1=wk(6, n),
                    op0=ALU.mult, op1=ALU.mult), R=[a3names[xs_i], wkr(6), "par"], W=[f"{hres}{kb}"])

        def branch_b(j, prefix, hview, hres, hi, rnd):
            if prefix:
                clo, chi = 0, TP
                npr = TP
                nsm = 0
            else:
                clo, chi = 32, E
                npr = 512
                nsm = 2
            ntok = npr + nsm * 16
            bi = j % 2
            gi = j % 4
            for jj in ([0, 1, 2] if j == 0 else ([j + 2] if j + 2 < 16 else [])):
                S.op("pool", lambda h, jj=jj: h.dma_start(out=wgj[:, (jj % 4) * 256:(jj % 4 + 1) * 256], in_=wg_d[jj]),
                     W=[f"wgj{jj % 4}"], dma=True)
            wt, wr = w_get(32 + j, 32)
            tx = ptile()
            mm_group(tx, clo, chi, [(wt[:, kb * 128:(kb + 1) * 128],
                                     (lambda plo, phi, kb=kb: hview[:, kb, plo:phi])) for kb in range(KB)],
                     R=[wr] + [f"{hres}{kb}" for kb in range(KB)])
            w_prefetch()
            if not prefix:
                wt2, wr2 = w_get(48 + j, 32)
                tg = ptile()
                mm_group(tg, FO, E, [(wt2[:, kb * 128:(kb + 1) * 128],
                                      (lambda plo, phi, kb=kb: hview[:, kb, plo:phi])) for kb in range(KB)],
                         R=[wr2] + [f"{hres}{kb}" for kb in range(KB)])
                w_prefetch()
            bx = wk(0 + bi)
            bxr = wkr(0 + bi)
            S.op("act", lambda h: h.activation(out=bx[:, 3:3 + npr], in_=psv(tx, clo, clo + npr), func=AF.Copy),
                 R=[f"ps{tx}"], W=[bxr])
            S.op("dve", lambda h: h.tensor_copy(out=bx[:, 0:3], in_=bxst[:, j * 3:j * 3 + 3]),
                 R=["bxst"], W=[bxr])
            if nsm:
                bxs = bx[:, 3 + npr:3 + npr + 38].rearrange("p (s k) -> p s k", k=19)
                S.op("act", lambda h: h.activation(
                    out=bxs[:, :, 3:19], in_=psv(tx, PE_, E).rearrange("p (s k) -> p s k", k=16), func=AF.Copy),
                    R=[f"ps{tx}"], W=[bxr])
                sbo = hi * 96 + j * 6
                S.op("dve", lambda h: h.tensor_copy(
                    out=bxs[:, :, 0:3], in_=sbv[:, sbo:sbo + 6].rearrange("p (s k) -> p s k", k=3)),
                    R=["sbv"], W=[bxr])
            xb = wk(2 + bi, ntok)
            xbr = wkr(2 + bi)
            wcol = P_WB + j * 4
            S.op("dve", lambda h: h.tensor_scalar(out=xb[:, 0:npr], in0=bx[:, 0:npr], scalar1=pc(wcol),
                                                  scalar2=pc(P_BB + j), op0=ALU.mult, op1=ALU.add),
                 R=[bxr, "par"], W=[xbr])
            for k in range(1, 4):
                S.op("dve", lambda h, k=k: h.scalar_tensor_tensor(
                    out=xb[:, 0:npr], in0=bx[:, k:k + npr], scalar=pc(wcol + k), in1=xb[:, 0:npr],
                    op0=ALU.mult, op1=ALU.add), R=[bxr, "par"], W=[xbr])
            if nsm:
                xbs = xb[:, npr:npr + 32].rearrange("p (s k) -> p s k", k=16)
                S.op("dve", lambda h: h.tensor_scalar(out=xbs, in0=bxs[:, :, 0:16], scalar1=pc(wcol),
                                                      scalar2=pc(P_BB + j), op0=ALU.mult, op1=ALU.add),
                     R=[bxr, "par"], W=[xbr])
                for k in range(1, 4):
                    S.op("dve", lambda h, k=k: h.scalar_tensor_tensor(
                        out=xbs, in0=bxs[:, :, k:k + 16], scalar=pc(wcol + k), in1=xbs,
                        op0=ALU.mult, op1=ALU.add), R=[bxr, "par"], W=[xbr])
            if prefix:
                S.op("dve", lambda h: h.tensor_copy(out=bxst[:, j * 3:j * 3 + 3], in_=bx[:, npr:npr + 3]),
                     R=[bxr], W=["bxst"])
            else:
                S.op("dve", lambda h: h.tensor_copy(out=bxst[:, j * 3:j * 3 + 3], in_=bx[:, npr:npr + 3]),
                     R=[bxr], W=["bxst"])
                ob = hi * 144 + j * 9
                S.op("dve", lambda h: h.tensor_copy(out=ocb[:, ob:ob + 3], in_=bx[:, npr:npr + 3]),
                     R=[bxr], W=["ocb"])
                S.op("dve", lambda h: h.tensor_copy(
                    out=ocb[:, ob + 3:ob + 9].rearrange("p (s k) -> p s k", k=3), in_=bxs[:, :, 16:19]),
                    R=[bxr], W=["ocb"])
            x16 = wk(4)[:, bi * 288:bi * 288 + 272].bitcast(BF16)
            x16r = wkr(4)
            S.op("act", lambda h: h.activation(out=x16[:, 0:ntok], in_=xb, func=AF.Copy), R=[xbr], W=[x16r])
            tr_ = ptile()
            ti_ = ptile()
            for (tt, go) in ((tr_, 0), (ti_, 128)):
                pcs = pieces(tt, 0, ntok)

                def fn(h, pcs=pcs, go=go):
                    ins = None
                    for (plo, phi, flat) in pcs:
                        ins = h.matmul(PS[:, flat:flat + (phi - plo)],
                                       lhsT=wgj[:, gi * 256 + go: gi * 256 + go + 128], rhs=x16[:, plo:phi],
                                       start=True, stop=True)
                    return ins
                S.op("pe", fn, R=[x16r, f"wgj{gi}"], W=[f"ps{tt}"])
            ab, a2b, vb, hbb = wk(7, ntok), wk(8, ntok), wk(9, ntok), wk(10, ntok)
            trb, trbr = a3(bi)[:, 0:ntok], a3names[bi]
            tib, tibr = a3(2 + bi)[:, 0:ntok], a3names[2 + bi]
            gl, glr = a3(4 + bi)[:, 0:NF], a3names[4 + bi]
            S.op("act", lambda h: h.activation(out=trb, in_=psv(tr_, 0, ntok), func=AF.Tanh,
                                               bias=der[:, 32 + j:33 + j], scale=0.5),
                 R=[f"ps{tr_}", "der"], W=[trbr])
            S.op("act", lambda h: h.activation(out=tib, in_=psv(ti_, 0, ntok), func=AF.Tanh,
                                               bias=der[:, 48 + j:49 + j], scale=0.5),
                 R=[f"ps{ti_}", "der"], W=[tibr])
            if not prefix:
                S.op("act", lambda h: h.activation(out=gl, in_=psv(tg, FO, E), func=AF.Gelu_apprx_tanh),
                     R=[f"ps{tg}"], W=[glr])
            yield
            S.op("act", lambda h: h.activation(out=ab, in_=trb, func=AF.Exp, bias=der[:, j:j + 1],
                                               scale=der[:, j:j + 1]), R=[trbr, "der"], W=[wkr(7)])
            S.op("act", lambda h: h.activation(out=a2b, in_=trb, func=AF.Exp, bias=der[:, 16 + j:17 + j],
                                               scale=der[:, 16 + j:17 + j]), R=[trbr, "der"], W=[wkr(8)])
            S.op("dve", lambda h: h.scalar_tensor_tensor(out=vb, in0=tib, scalar=1.0, in1=xb, op0=ALU.add,
                                                         op1=ALU.mult), R=[tibr, xbr], W=[wkr(9)])
            S.op("act", lambda h: h.activation(out=a2b, in_=a2b, func=AF.Sqrt, bias=epsb[:, 1:2], scale=-1.0),
                 R=[wkr(8), "epsb"], W=[wkr(8)])
            S.op("dve", lambda h: h.scalar_tensor_tensor(out=vb, in0=a2b, scalar=0.5, in1=vb, op0=ALU.mult,
                                                         op1=ALU.mult), R=[wkr(8), wkr(9)], W=[wkr(9)])
            if prefix:
                init = hraw[:, j:j + 1]
                initr = "hraw"
            else:
                init = h0b[:, j:j + 1]
                initr = "h0b"
            S.op("dve", lambda h: h.tensor_tensor_scan(out=hbb[:, 0:npr], data0=ab[:, 0:npr], data1=vb[:, 0:npr],
                                                       initial=init, op0=ALU.mult, op1=ALU.add),
                 R=[wkr(7), wkr(9), initr], W=[wkr(10)])
            for s in range(nsm):
                so = hi * 32 + j * 2 + s
                S.op("dve", lambda h, s=s, so=so: h.tensor_tensor_scan(
                    out=hbb[:, npr + 16 * s:npr + 16 * s + 16], data0=ab[:, npr + 16 * s:npr + 16 * s + 16],
                    data1=vb[:, npr + 16 * s:npr + 16 * s + 16], initial=shs[:, so:so + 1],
                    op0=ALU.mult, op1=ALU.add), R=[wkr(7), wkr(9), "shs"], W=[wkr(10)])
            if prefix:
                S.op("dve", lambda h: h.tensor_copy(out=hraw[:, j:j + 1], in_=hbb[:, npr - 1:npr]),
                     R=[wkr(10)], W=["hraw"])
                if rnd == 1:
                    S.op("dve", lambda h: h.tensor_copy(out=hbhl[:, 2 * j:2 * j + 2], in_=hbb[:, npr - 2:npr]),
                         R=[wkr(10)], W=["hbhl"])
                    S.op("dve", lambda h: h.tensor_scalar(out=h0b[:, j:j + 1], in0=hbb[:, npr - 1:npr],
                                                          scalar1=msk[:, 0:1], scalar2=None, op0=ALU.mult),
                         R=[wkr(10), "msk"], W=["h0b"])
            else:
                S.op("dve", lambda h: h.tensor_tensor(out=yg_v[:, 16 + j, 0:2], in0=gl[:, 0:2],
                                                      in1=hbhl[:, 2 * j:2 * j + 2], op=ALU.mult),
                     R=[glr, "hbhl"], W=[f"yg{16 + j}"])
                S.op("dve", lambda h: h.tensor_tensor(out=yg_v[:, 16 + j, 2:NF], in0=gl[:, 2:NF], in1=hbb,
                                                      op=ALU.mult), R=[glr, wkr(10)], W=[f"yg{16 + j}"])
                ol = hi * 48 + j * 3
                S.op("dve", lambda h: h.tensor_copy(out=olr[:, ol:ol + 1], in_=hbb[:, npr - 1:npr]),
                     R=[wkr(10)], W=["olr"])
                S.op("dve", lambda h: h.tensor_copy(out=olr[:, ol + 1:ol + 2], in_=hbb[:, npr + 15:npr + 16]),
                     R=[wkr(10)], W=["olr"])
                S.op("dve", lambda h: h.tensor_copy(out=olr[:, ol + 2:ol + 3], in_=hbb[:, npr + 31:npr + 32]),
                     R=[wkr(10)], W=["olr"])
                S.op("dve", lambda h: h.tensor_copy(out=hbhl[:, 2 * j:2 * j + 2], in_=hbb[:, npr - 2:npr]),
                     R=[wkr(10)], W=["hbhl"])
                S.op("dve", lambda h: h.tensor_copy(out=h0b[:, j:j + 1], in_=hbb[:, npr - 1:npr]),
                     R=[wkr(10)], W=["h0b"])

        def stage(n):
            if STAGE < n:
                raise _Stop()

        def run_bb(prefix, hview, hres_, hi, rnd):
            prev = None
            for j in range(16):
                g = branch_b(j, prefix, hview, hres_, hi, rnd)
                next(g)
                if prev is not None:
                    for _ in prev:
                        pass
                prev = g
            for _ in prev:
                pass

        def one_half(hi):
            stage(2 if hi == 0 else 7)
            inject = [(S.csem[e], S.cnt[e]) for e in ("pe",) if S.cnt[e] > 0]
            for nm in a3names:
                d = S.readers.setdefault(nm, {})
                for (k, v) in inject:
                    d[k] = max(d.get(k, 0), v)
            rmsnorm_in(xm[hi], E, h_v, "h", None)
            hres = [f"h{kb}" for kb in range(KB)]

            sa_flat = A3F[:, 9 * WB:9 * WB + 960]
            sa_all = sa_flat.rearrange("p (j c) -> p j c", c=60)
            S.op("sp", lambda h, hi=hi: h.dma_start(out=sa_flat, in_=sa_d[hi]),
                 W=["a3sa"] + a3names, dma=True)
            NOFF = int(os.environ.get("MK_NOFF", "0"))
            off_taps = list(range(0, NOFF))
            dve_taps = list(range(NOFF, 31))
            NO = 606
            for j in range(16):
                bi = j % 2
                wt, wr = w_get(j, 32)
                tv = ptile()
                mm_group(tv, 0, E, [(wt[:, kb * 128:(kb + 1) * 128],
                                     (lambda plo, phi, kb=kb: h_v[:, kb, plo:phi])) for kb in range(KB)],
                         R=[wr] + hres)
                w_prefetch()
                wt2, wr2 = w_get(16 + j, 32)
                tg = ptile()
                mm_group(tg, 0, E, [(wt2[:, kb * 128:(kb + 1) * 128],
                                     (lambda plo, phi, kb=kb: h_v[:, kb, plo:phi])) for kb in range(KB)],
                         R=[wr2] + hres)
                w_prefetch()
                sg = wk(0 + bi)
                S.op("act", lambda h, sg=sg, tg=tg: h.activation(out=sg, in_=psv(tg, 0, E), func=AF.Sigmoid),
                     R=[f"ps{tg}"], W=[wkr(0 + bi)])
                ua = a3(bi)
                uar = a3names[bi]
                uas = ua[:, 544:636].rearrange("p (s k) -> p s k", k=46)
                S.op("dve", lambda h, ua=ua, sg=sg, tv=tv: h.tensor_tensor(
                    out=ua[:, 0:PE_], in0=psv(tv, 0, PE_), in1=sg[:, 0:PE_], op=ALU.mult),
                    R=[f"ps{tv}", wkr(0 + bi)], W=[uar])
                S.op("dve", lambda h, uas=uas, sg=sg, tv=tv: h.tensor_tensor(
                    out=uas[:, :, 30:46], in0=psv(tv, PE_, E).rearrange("p (s k) -> p s k", k=16),
                    in1=sg[:, PE_:E].rearrange("p (s k) -> p s k", k=16), op=ALU.mult),
                    R=[f"ps{tv}", wkr(0 + bi)], W=[uar])
                S.op("act", lambda h, uas=uas, j=j: h.activation(
                    out=uas[:, :, 0:30], in_=sa_all[:, j, :].rearrange("p (s k) -> p s k", k=30), func=AF.Copy),
                    R=["a3sa"], W=[uar])
                wcol = P_WA + j * 31
                ypi = 4 if bi == 0 else 7
                yP, yPr = a3(ypi), a3names[ypi]
                for n_, k in enumerate(off_taps):
                    if n_ == 0:
                        S.op("act", lambda h, ua=ua, k=k, wcol=wcol, yP=yP: h.activation(
                            out=yP[:, 0:NO], in_=ua[:, k:k + NO], func=AF.Identity, scale=pc(wcol + k)),
                            R=[uar, "par"], W=[yPr])
                    else:
                        tb_i = 5 + n_ % 2
                        S.op("act", lambda h, ua=ua, k=k, wcol=wcol, tb_i=tb_i: h.activation(
                            out=a3(tb_i)[:, 0:NO], in_=ua[:, k:k + NO], func=AF.Identity, scale=pc(wcol + k)),
                            R=[uar, "par"], W=[a3names[tb_i]])
                        S.op("pool", lambda h, yP=yP, tb_i=tb_i: h.tensor_tensor(
                            out=yP[:, 0:NO], in0=yP[:, 0:NO], in1=a3(tb_i)[:, 0:NO], op=ALU.add),
                            R=[a3names[tb_i]], W=[yPr])
                accs = [(a3(2), a3names[2]), (a3(3), a3names[3])]
                for n_, k in enumerate(dve_taps):
                    ya, yar = accs[n_ % 2]
                    if n_ == 0:
                        S.op("dve", lambda h, ya=ya, ua=ua, k=k, wcol=wcol, j=j: h.tensor_scalar(
                            out=ya[:, 0:NO], in0=ua[:, k:k + NO], scalar1=pc(wcol + k), scalar2=pc(P_BA + j),
                            op0=ALU.mult, op1=ALU.add), R=[uar, "par"], W=[yar])
                    elif n_ == 1:
                        S.op("dve", lambda h, ya=ya, ua=ua, k=k, wcol=wcol: h.tensor_scalar(
                            out=ya[:, 0:NO], in0=ua[:, k:k + NO], scalar1=pc(wcol + k), scalar2=None,
                            op0=ALU.mult), R=[uar, "par"], W=[yar])
                    else:
                        S.op("dve", lambda h, ya=ya, ua=ua, k=k, wcol=wcol: h.scalar_tensor_tensor(
                            out=ya[:, 0:NO], in0=ua[:, k:k + NO], scalar=pc(wcol + k), in1=ya[:, 0:NO],
                            op0=ALU.mult, op1=ALU.add), R=[uar, "par"], W=[yar])
                S.op("dve", lambda h, accs=accs: h.tensor_tensor(out=accs[0][0][:, 0:NO], in0=accs[0][0][:, 0:NO],
                                                                 in1=accs[1][0][:, 0:NO], op=ALU.add),
                     R=[accs[1][1]], W=[accs[0][1]])
                S.op("sp", lambda h, ua=ua, j=j, hi=hi: h.dma_start(out=oca_d[hi, j][:, 0:30], in_=ua[:, 514:544]),
                     R=[uar], dma=True)
                S.op("sp", lambda h, uas=uas, j=j, hi=hi: h.dma_start(
                    out=oca_d[hi, j][:, 30:90].rearrange("p (s k) -> p s k", k=30), in_=uas[:, :, 16:46]),
                    R=[uar], dma=True)
                yb = wk(5 + bi, NF)
                ybr = wkr(5 + bi)
                yA = accs[0][0]
                if NOFF > 0:
                    S.op("pool", lambda h, yb=yb, yA=yA, yP=yP: h.tensor_tensor(
                        out=yb[:, 0:514], in0=yA[:, 0:514], in1=yP[:, 0:514], op=ALU.add),
                        R=[accs[0][1], yPr], W=[ybr])
                    S.op("pool", lambda h, yb=yb, yA=yA, yP=yP: h.tensor_tensor(
                        out=yb[:, 514:546].rearrange("p (s k) -> p s k", k=16),
                        in0=yA[:, 544:636].rearrange("p (s k) -> p s k", k=46)[:, :, 0:16],
                        in1=yP[:, 544:636].rearrange("p (s k) -> p s k", k=46)[:, :, 0:16], op=ALU.add),
                        R=[accs[0][1], yPr], W=[ybr])
                else:
                    S.op("act", lambda h, yb=yb, yA=yA: h.activation(out=yb[:, 0:514], in_=yA[:, 0:514],
                                                                     func=AF.Copy), R=[accs[0][1]], W=[ybr])
                    S.op("act", lambda h, yb=yb, yA=yA: h.activation(
                        out=yb[:, 514:546].rearrange("p (s k) -> p s k", k=16),
                        in_=yA[:, 544:636].rearrange("p (s k) -> p s k", k=46)[:, :, 0:16], func=AF.Copy),
                        R=[accs[0][1]], W=[ybr])
                S.op("act", lambda h, yb=yb: h.activation(out=wk(7, NF), in_=yb, func=AF.Square),
                     R=[ybr], W=[wkr(7)])
                if j == 0:
                    S.op("pool", lambda h, yb=yb: h.tensor_copy(out=wk(8, NF), in_=yb), R=[ybr], W=[wkr(8)])
                    S.op("pool", lambda h: h.tensor_copy(out=wk(9, NF), in_=wk(7, NF)), R=[wkr(7)], W=[wkr(9)])
                else:
                    S.op("pool", lambda h, yb=yb: h.tensor_tensor(out=wk(8, NF), in0=wk(8, NF), in1=yb, op=ALU.add),
                         R=[ybr], W=[wkr(8)])
                    S.op("pool", lambda h: h.tensor_tensor(out=wk(9, NF), in0=wk(9, NF), in1=wk(7, NF), op=ALU.add),
                         R=[wkr(7)], W=[wkr(9)])
                S.op("act", lambda h, yb=yb, j=j: h.activation(out=yg_v[:, j, :], in_=yb, func=AF.Copy),
                     R=[ybr], W=[f"yg{j}"])
            bcast_sum(wk(8, NF), wkr(8), NF, wk(10, NF), wkr(10), 1.0 / 2048)
            bcast_sum(wk(9, NF), wkr(9), NF, wk(11, NF), wkr(11), 1.0 / 2048)
            S.op("dve", lambda h: h.tensor_tensor(out=wk(7, NF), in0=wk(10, NF), in1=wk(10, NF), op=ALU.mult),
                 R=[wkr(10)], W=[wkr(7)])
            S.op("dve", lambda h: h.tensor_tensor(out=wk(11, NF), in0=wk(11, NF), in1=wk(7, NF), op=ALU.subtract),
                 R=[wkr(7), wkr(11)], W=[wkr(11)])
            S.op("act", lambda h: h.activation(out=wk(11, NF), in_=wk(11, NF), func=AF.Sqrt, bias=epsb[:, 0:1],
                                               scale=1.0), R=[wkr(11), "epsb"], W=[wkr(11)])
            S.op("dve", lambda h: h.reciprocal(out=wk(11, NF), in_=wk(11, NF)), R=[wkr(11)], W=[wkr(11)])
            S.op("dve", lambda h: h.scalar_tensor_tensor(out=wk(7, NF), in0=wk(10, NF), scalar=-1.0, in1=wk(11, NF),
                                                         op0=ALU.mult, op1=ALU.mult),
                 R=[wkr(10), wkr(11)], W=[wkr(7)])
            for j in range(16):
                tb = wk(j % 2, NF)
                tbr = wkr(j % 2)
                S.op("dve", lambda h, tb=tb, j=j: h.tensor_tensor(out=tb, in0=yg_v[:, j, :], in1=wk(11, NF),
                                                                 op=ALU.mult), R=[f"yg{j}", wkr(11)], W=[tbr])
                S.op("dve", lambda h, tb=tb: h.tensor_tensor(out=tb, in0=tb, in1=wk(7, NF), op=ALU.add),
                     R=[wkr(7)], W=[tbr])
                S.op("act", lambda h, tb=tb, j=j: h.activation(out=yg_v[:, j, :], in_=tb, func=AF.Silu,
                                                               bias=pc(P_LB + j), scale=pc(P_LG + j)),
                     R=[tbr, "par"], W=[f"yg{j}"])
            stage(3)
            run_bb(False, h_v, "h", hi, 0)
            stage(4)
            inject = [(S.csem[e], S.cnt[e]) for e in ("act", "dve", "pool")]
            for kb in range(KB):
                d = S.readers.setdefault(f"mx{kb}", {})
                for (k, v) in inject:
                    d[k] = max(d.get(k, 0), v)
            for j in range(32):
                bi = j % 2
                wab, wabr = w_get(CID_AB + j, 32)
                ta = ptile()
                mm_group(ta, FO, E, [(wab[:, kb * 128:(kb + 1) * 128],
                                      (lambda plo, phi, kb=kb: yg_v[:, kb, plo - FO:phi - FO])) for kb in range(16)],
                         R=[wabr] + [f"yg{kb}" for kb in range(16)])
                tbb = ptile()
                mm_group(tbb, FO, E, [(wab[:, (16 + kb) * 128:(17 + kb) * 128],
                                       (lambda plo, phi, kb=kb: yg_v[:, 16 + kb, plo - FO:phi - FO]))
                                      for kb in range(16)],
                         R=[wabr] + [f"yg{16 + kb}" for kb in range(16)])
                w_prefetch()
                wma, wmar = w_get(64 + j, 32)
                tma = ptile()
                mm_group(tma, FO, E, [(wma[:, kb * 128:(kb + 1) * 128],
                                       (lambda plo, phi, kb=kb: h_v[:, kb, plo:phi])) for kb in range(KB)],
                         R=[wmar] + hres)
                w_prefetch()
                wmb, wmbr = w_get(96 + j, 32)
                tmb = ptile()
                mm_group(tmb, FO, E, [(wmb[:, kb * 128:(kb + 1) * 128],
                                       (lambda plo, phi, kb=kb: h_v[:, kb, plo:phi])) for kb in range(KB)],
                         R=[wmbr] + hres)
                w_prefetch()
                sga, sgb, t1, t2 = wk(0 + bi, NF), wk(2 + bi, NF), wk(4 + bi, NF), wk(6 + bi, NF)
                S.op("act", lambda h, sga=sga, tma=tma: h.activation(out=sga, in_=psv(tma, FO, E), func=AF.Sigmoid),
                     R=[f"ps{tma}"], W=[wkr(0 + bi)])
                S.op("dve", lambda h, t1=t1, sga=sga, ta=ta: h.tensor_tensor(out=t1, in0=psv(ta, FO, E), in1=sga,
                                                                             op=ALU.mult),
                     R=[f"ps{ta}", wkr(0 + bi)], W=[wkr(4 + bi)])
                S.op("act", lambda h, sgb=sgb, tmb=tmb: h.activation(out=sgb, in_=psv(tmb, FO, E), func=AF.Sigmoid),
                     R=[f"ps{tmb}"], W=[wkr(2 + bi)])
                S.op("dve", lambda h, t2=t2, sgb=sgb, tbb=tbb: h.tensor_tensor(out=t2, in0=psv(tbb, FO, E), in1=sgb,
                                                                               op=ALU.mult),
                     R=[f"ps{tbb}", wkr(2 + bi)], W=[wkr(6 + bi)])
                S.op("pool", lambda h, t1=t1, t2=t2, j=j: h.tensor_tensor(out=mx_v[:, j, :], in0=t1, in1=t2,
                                                                         op=ALU.add),
                     R=[wkr(4 + bi), wkr(6 + bi)], W=[f"mx{j}"])
            stage(5)
            for j in range(32):
                bi = j % 2
                wo, wor = w_get(CID_WO + j, 32)
                to = ptile()
                mm_group(to, FO, E, [(wo[:, kb * 128:(kb + 1) * 128],
                                      (lambda plo, phi, kb=kb: mx_v[:, kb, plo - FO:phi - FO])) for kb in range(KB)],
                         R=[wor] + [f"mx{kb}" for kb in range(KB)])
                w_prefetch()
                S.op("act", lambda h, to=to, j=j: h.activation(out=o32_v[:, j, :], in_=psv(to, FO, E), func=AF.Copy),
                     R=[f"ps{to}"], W=[f"o{j}"] + hres + [f"yg{kb}" for kb in range(KB)])
                S.op("act", lambda h, to=to, bi=bi: h.activation(out=wk(0 + bi, NF), in_=psv(to, FO, E),
                                                                 func=AF.Square),
                     R=[f"ps{to}"], W=[wkr(0 + bi)])
                if j == 0:
                    S.op("pool", lambda h, bi=bi: h.tensor_copy(out=wk(2, NF), in_=wk(0 + bi, NF)),
                         R=[wkr(0 + bi)], W=[wkr(2)])
                else:
                    S.op("pool", lambda h, bi=bi: h.tensor_tensor(out=wk(2, NF), in0=wk(2, NF), in1=wk(0 + bi, NF),
                                                                  op=ALU.add), R=[wkr(0 + bi)], W=[wkr(2)])
            bcast_sum(wk(2, NF), wkr(2), NF, wk(3, NF), wkr(3), 1.0 / D, EPS, -0.5)
            def xload(jj):
                xi_ = 4 + jj % 4
                S.op("sp", lambda h, jj=jj, xi_=xi_, hi=hi: h.dma_start(out=wk(xi_, NF), in_=xm[hi, jj][:, FO:E]),
                     W=[wkr(xi_)], dma=True)
            for jj in range(4):
                xload(jj)
            for j in range(32):
                xs_i = 4 + j % 4
                bi = j % 2
                S.op("dve", lambda h, j=j: h.scalar_tensor_tensor(
                    out=o32_v[:, j, :], in0=o32_v[:, j, :], scalar=pc(P_GPM + j), in1=wk(3, NF),
                    op0=ALU.mult, op1=ALU.mult), R=[wkr(3), "par"], W=[f"o{j}"])
                S.op("dve", lambda h, j=j, xs_i=xs_i: h.tensor_tensor(out=o32_v[:, j, :], in0=o32_v[:, j, :],
                                                                      in1=wk(xs_i, NF), op=ALU.add),
                     R=[wkr(xs_i)], W=[f"o{j}"])
                S.op("sp", lambda h, j=j, hi=hi: h.dma_start(out=xmid_d[hi, j], in_=o32_v[:, j, :]),
                     R=[f"o{j}"], W=[f"xmid{j}"], dma=True)
                if j + 4 < 32:
                    xload(j + 4)
                S.op("act", lambda h, j=j, bi=bi: h.activation(out=wk(8 + bi, NF), in_=o32_v[:, j, :],
                                                               func=AF.Square), R=[f"o{j}"], W=[wkr(8 + bi)])
                if j == 0:
                    S.op("pool", lambda h, bi=bi: h.tensor_copy(out=wk(10, NF), in_=wk(8 + bi, NF)),
                         R=[wkr(8 + bi)], W=[wkr(10)])
                else:
                    S.op("pool", lambda h, bi=bi: h.tensor_tensor(out=wk(10, NF), in0=wk(10, NF),
                                                                  in1=wk(8 + bi, NF), op=ALU.add),
                         R=[wkr(8 + bi)], W=[wkr(10)])
            bcast_sum(wk(10, NF), wkr(10), NF, wk(11, NF), wkr(11), 1.0 / D, EPS, -0.5)
            for j in range(32):
                S.op("dve", lambda h, j=j: h.scalar_tensor_tensor(
                    out=mx_v[:, j, :], in0=o32_v[:, j, :], scalar=pc(P_GPF + j), in1=wk(11, NF),
                    op0=ALU.mult, op1=ALU.mult), R=[f"o{j}", wkr(11), "par"], W=[f"mx{kb}" for kb in range(KB)])
                S.op("dve", lambda h, j=j, hi=hi: h.tensor_scalar(
                    out=mx_v[:, j, 0:2], in0=mx_v[:, j, 0:2], scalar1=msk[:, hi:hi + 1], scalar2=None,
                    op0=ALU.mult), R=["msk"], W=[f"mx{kb}" for kb in range(KB)])
            stage(6)
            h2res = [f"mx{kb}" for kb in range(KB)]
            allo = [f"o{j}" for j in range(32)]
            for m in range(NFF):
                bi = m % 2
                cs = []
                for half_, ch in ((0, m), (1, NFF + m)):
                    wu, wur = w_get(CID_UP + ch, 32)
                    tu = ptile()
                    mm_group(tu, FO, E, [(wu[:, kb * 128:(kb + 1) * 128],
                                          (lambda plo, phi, kb=kb: mx_v[:, kb, plo - FO:phi - FO]))
                                         for kb in range(KB)], R=[wur] + h2res)
                    w_prefetch()
                    cb_i = (0 if half_ == 0 else 2) + bi
                    cbuf = wk(cb_i, NB)
                    wcol = P_WF + ch * 3
                    S.op("dve", lambda h, cbuf=cbuf, tu=tu, wcol=wcol, ch=ch: h.tensor_scalar(
                        out=cbuf[:, 0:512], in0=psv(tu, FO, FO + 512), scalar1=pc(wcol), scalar2=pc(P_BF + ch),
                        op0=ALU.mult, op1=ALU.add), R=[f"ps{tu}", "par"], W=[wkr(cb_i)])
                    for k in (1, 2):
                        S.op("dve", lambda h, cbuf=cbuf, tu=tu, wcol=wcol, k=k: h.scalar_tensor_tensor(
                            out=cbuf[:, 0:512], in0=psv(tu, FO + k, FO + k + 512), scalar=pc(wcol + k),
                            in1=cbuf[:, 0:512], op0=ALU.mult, op1=ALU.add), R=[f"ps{tu}", "par"], W=[wkr(cb_i)])
                    spb = wk(6)[:, (half_ * 2 + bi) * 40:(half_ * 2 + bi) * 40 + 36].rearrange("p (s k) -> p s k", k=18)
                    spr = wkr(6)
                    S.op("act", lambda h, spb=spb, tu=tu: h.activation(
                        out=spb[:, :, 2:18], in_=psv(tu, PE_, E).rearrange("p (s k) -> p s k", k=16), func=AF.Copy),
                        R=[f"ps{tu}"], W=[spr])
                    so = (hi * 172 + ch) * 4
                    S.op("act", lambda h, spb=spb, so=so: h.activation(
                        out=spb[:, :, 0:2], in_=sfs[:, so:so + 4].rearrange("p (s k) -> p s k", k=2), func=AF.Copy),
                        R=["sfs"], W=[spr])
                    cbs = cbuf[:, 512:544].rearrange("p (s k) -> p s k", k=16)
                    S.op("dve", lambda h, cbs=cbs, spb=spb, wcol=wcol, ch=ch: h.tensor_scalar(
                        out=cbs, in0=spb[:, :, 0:16], scalar1=pc(wcol), scalar2=pc(P_BF + ch),
                        op0=ALU.mult, op1=ALU.add), R=[spr, "par"], W=[wkr(cb_i)])
                    for k in (1, 2):
                        S.op("dve", lambda h, cbs=cbs, spb=spb, wcol=wcol, k=k: h.scalar_tensor_tensor(
                            out=cbs, in0=spb[:, :, k:k + 16], scalar=pc(wcol + k), in1=cbs,
                            op0=ALU.mult, op1=ALU.add), R=[spr, "par"], W=[wkr(cb_i)])
                    fo = (hi * 172 + ch) * 6
                    S.op("act", lambda h, tu=tu, fo=fo: h.activation(out=off[:, fo:fo + 2], in_=psv(tu, PE_ - 2, PE_),
                                                                     func=AF.Copy), R=[f"ps{tu}"], W=["off"])
                    S.op("act", lambda h, spb=spb, fo=fo: h.activation(
                        out=off[:, fo + 2:fo + 6].rearrange("p (s k) -> p s k", k=2), in_=spb[:, :, 16:18],
                        func=AF.Copy), R=[spr], W=["off"])
                    cs.append((cbuf, cb_i))
                ga = wk(4 + bi, NB)
                S.op("act", lambda h, ga=ga, c0=cs[0][0]: h.activation(out=ga, in_=c0, func=AF.Gelu_apprx_tanh),
                     R=[wkr(cs[0][1])], W=[wkr(4 + bi)])
                S.op("dve", lambda h, ga=ga, c1=cs[1][0], m=m: h.tensor_tensor(out=fin_v[:, m, :], in0=ga, in1=c1,
                                                                              op=ALU.mult),
                     R=[wkr(4 + bi), wkr(cs[1][1])], W=[f"fin{m}"] + (allo + hres if False else allo))
            finres = [f"fin{m}" for m in range(NFF)]
            inj3 = [(S.csem[e], S.cnt[e]) for e in ("pe", "act", "dve", "pool")]
            for nm in a3names:
                d = S.readers.setdefault(nm, {})
                for (k, v) in inj3:
                    d[k] = max(d.get(k, 0), v)

            def fslots(jj):
                s8 = jj % 8
                if s8 < 4:
                    return wk(s8, NB), wkr(s8), wk(8 + s8, NB), wkr(8 + s8)
                return a3(s8 - 4)[:, 0:NB], a3names[s8 - 4], a3(s8)[:, 0:NB], a3names[s8]

            def final_loads(jj):
                fb, fbr, xb_, xbr_ = fslots(jj)
                if jj >= 4 or True:
                    S.op("sp", lambda h, fb=fb, jj=jj, hi=hi: h.dma_start(out=fb, in_=f_d[hi, jj][:, 0:NB]),
                         R=[f"fd{jj}"], W=[fbr], dma=True)
                    S.op("sp", lambda h, xb_=xb_, jj=jj, hi=hi: h.dma_start(out=xb_, in_=xmid_d[hi, jj][:, 2:NF]),
                         R=[f"xmid{jj}"], W=[xbr_], dma=True)

            for j in range(32):
                bi = j % 2
                if 5 <= j <= 8:
                    final_loads(j - 1)
                tf = ptile()
                pcs_f = pieces(tf, FO + 2, E)
                for part, nkb in ((0, 32), (1, 32), (2, 22)):
                    wd, wdr = w_get(CID_DN + 3 * j + part, nkb)

                    def fn(h, wd=wd, part=part, nkb=nkb, pcs_f=pcs_f):
                        ins = None
                        for kb in range(nkb):
                            m = part * 32 + kb
                            for (plo, phi, flat) in pcs_f:
                                ins = h.matmul(PS[:, flat:flat + (phi - plo)], lhsT=wd[:, kb * 128:(kb + 1) * 128],
                                               rhs=fin_v[:, m, plo - FO - 2:phi - FO - 2],
                                               start=(m == 0), stop=(m == NFF - 1))
                        return ins
                    S.op("pe", fn, R=[wdr] + finres[part * 32:part * 32 + nkb], W=[f"ps{tf}"])
                    w_prefetch()
                fb = wk(8 + bi, NB)
                S.op("act", lambda h, fb=fb, tf=tf: h.activation(out=fb, in_=psv(tf, FO + 2, E), func=AF.Copy),
                     R=[f"ps{tf}"], W=[wkr(8 + bi)])
                S.op("sp", lambda h, fb=fb, j=j, hi=hi: h.dma_start(out=f_d[hi, j][:, 0:NB], in_=fb),
                     R=[wkr(8 + bi)], W=[f"fd{j}"], dma=True)
                S.op("act", lambda h, tf=tf: h.activation(out=wk(10, NB), in_=psv(tf, FO + 2, E), func=AF.Square),
                     R=[f"ps{tf}"], W=[wkr(10)])
                if j == 0:
                    S.op("pool", lambda h: h.tensor_copy(out=wk(11, NB), in_=wk(10, NB)), R=[wkr(10)], W=[wkr(11)])
                else:
                    S.op("pool", lambda h: h.tensor_tensor(out=wk(11, NB), in0=wk(11, NB), in1=wk(10, NB), op=ALU.add),
                         R=[wkr(10)], W=[wkr(11)])
            bcast_sum(wk(11, NB), wkr(11), NB, wk(7, NB), wkr(7), 1.0 / D, EPS, -0.5)
            for j in range(32):
                bi = j % 2
                fb, fbr, xb_, xbr_ = fslots(j)
                if j == 0:
                    for jj in range(4):
                        final_loads(jj)
                S.op("dve", lambda h, fb=fb, j=j: h.scalar_tensor_tensor(
                    out=fb, in0=fb, scalar=pc(P_GPO + j), in1=wk(7, NB), op0=ALU.mult, op1=ALU.mult),
                    R=[wkr(7), "par"], W=[fbr])
                S.op("dve", lambda h, fb=fb, xb_=xb_: h.tensor_tensor(out=fb, in0=fb, in1=xb_, op=ALU.add),
                     R=[xbr_], W=[fbr])
                S.op("sp", lambda h, fb=fb, j=j, hi=hi: h.dma_start(out=y_d[hi, j], in_=fb),
                     R=[fbr], dma=True)
                if j >= 1 and j + 7 < 32:
                    final_loads(j + 7)
            for kb in range(KB):
                S.readers.setdefault(f"h{kb}", {})
                S.readers.setdefault(f"yg{kb}", {})
            lastpe = (S.csem["pe"], S.cnt["pe"])
            lastdve = (S.csem["dve"], S.cnt["dve"])
            lastact = (S.csem["act"], S.cnt["act"])
            for kb in range(KB):
                for nm in (f"h{kb}", f"yg{kb}"):
                    d = S.readers.setdefault(nm, {})
                    for (k, v) in (lastpe, lastdve, lastact):
                        d[k] = max(d.get(k, 0), v)

        try:
            stage(1)
            sub = int(os.environ.get("MK_SUB", "99"))
            for rnd in range(2):
                rmsnorm_in(xp[rnd], TP, hp_v, "yg", None)
                if sub == 1:
                    raise _Stop()
                run_bb(True, hp_v, "yg", 0, rnd)
            for hi in range(2):
                one_half(hi)
        except _Stop:
            pass

        S.op("sp", lambda h: h.dma_start(out=ocb_d, in_=ocb[:, :]), R=["ocb"], dma=True)
        S.op("sp", lambda h: h.dma_start(out=olr_d, in_=olr[:, :]), R=["olr"], dma=True)
        S.op("sp", lambda h: h.dma_start(out=off_d, in_=off[:, :]), R=["off"], dma=True)
        S.final_wait("sp")

        assert dry or wst["use"] == len(plan), (wst, len(plan))

        with nc.Block() as block:
            @block.tensor
            def _(t):
                for f in S.streams["pe"]:
                    f(t)

            @block.scalar
            def _(a):
                for f in S.streams["act"]:
                    f(a)

            @block.vector
            def _(v):
                for f in S.streams["dve"]:
                    f(v)

            @block.gpsimd
            def _(g):
                for f in S.streams["pool"]:
                    f(g)

            @block.sync
            def _(s):
                for f in S.streams["sp"]:
                    f(s)
    return nc


def _tile_w(W):
    K, N = W.shape
    kb, nb = K // 128, N // 128
    return np.ascontiguousarray(W.reshape(kb, 128, nb, 128).transpose(2, 1, 0, 3)).reshape(nb, 128, kb * 128)


def _fm(v, nblk):
    return np.ascontiguousarray(v.reshape(nblk, 128).T)


def _tok_fm(a):
    T, C = a.shape
    return np.ascontiguousarray(a.T).reshape(C // 128, 128, T)


_NC_CACHE = {}


def kernel(x_prompt, x_sample, state_conv_a, state_conv_b, state_lru, state_ffn,
           g_pre_mix, w_in, w_dw_a, b_dw_a, ln_a_g, ln_a_b, w_a_out,
           w_dw_b, b_dw_b, w_rg_r, b_rg_r, w_rg_i, b_rg_i, lru_lambda, w_b_out,
           w_o, g_post_mix, g_pre_ffn, w_up, w_dw_f, b_dw_f, w_down, g_post_ffn):
    f = np.float32
    x_prompt = np.asarray(x_prompt, f)
    x_sample = np.asarray(x_sample, f)
    wall = np.zeros((NCH, 128, 4096), f)
    wall[0:128] = _tile_w(np.asarray(w_in[0], f))
    wall[CID_AB:CID_AB + 32, :, 0:2048] = _tile_w(np.asarray(w_a_out[0], f))
    wall[CID_AB:CID_AB + 32, :, 2048:4096] = _tile_w(np.asarray(w_b_out[0], f))
    wall[CID_WO:CID_WO + 32] = _tile_w(np.asarray(w_o[0], f))
    wall[CID_UP:CID_UP + 172] = _tile_w(np.asarray(w_up[0], f))
    wd = _tile_w(np.asarray(w_down[0], f))
    dn = wall[CID_DN:CID_DN + 96].reshape(32, 3, 128, 4096)
    dn[:, 0] = wd[:, :, 0:4096]
    dn[:, 1] = wd[:, :, 4096:8192]
    dn[:, 2, :, 0:2816] = wd[:, :, 8192:11008]
    wg = np.concatenate([np.asarray(w_rg_r[0], f), np.asarray(w_rg_i[0], f)], axis=2)
    par = np.zeros((128, NPAR), f)
    par[:, P_GPRE:P_GPRE + 32] = _fm(np.asarray(g_pre_mix[0], f), 32)
    par[:, P_WA:P_WA + 496] = np.asarray(w_dw_a[0], f).reshape(31, 16, 128).transpose(2, 1, 0).reshape(128, 496)
    par[:, P_BA:P_BA + 16] = _fm(np.asarray(b_dw_a[0], f), 16)
    par[:, P_LG:P_LG + 16] = _fm(np.asarray(ln_a_g[0], f), 16)
    par[:, P_LB:P_LB + 16] = _fm(np.asarray(ln_a_b[0], f), 16)
    par[:, P_WB:P_WB + 64] = np.asarray(w_dw_b[0], f).reshape(4, 16, 128).transpose(2, 1, 0).reshape(128, 64)
    par[:, P_BB:P_BB + 16] = _fm(np.asarray(b_dw_b[0], f), 16)
    par[:, P_BR:P_BR + 16] = _fm(np.asarray(b_rg_r[0], f), 16)
    par[:, P_BI:P_BI + 16] = _fm(np.asarray(b_rg_i[0], f), 16)
    par[:, P_LAM:P_LAM + 16] = _fm(np.asarray(lru_lambda[0], f), 16)
    par[:, P_GPM:P_GPM + 32] = _fm(np.asarray(g_post_mix[0], f), 32)
    par[:, P_GPF:P_GPF + 32] = _fm(np.asarray(g_pre_ffn[0], f), 32)
    par[:, P_WF:P_WF + 516] = np.asarray(w_dw_f[0], f).reshape(3, 172, 128).transpose(2, 1, 0).reshape(128, 516)
    par[:, P_BF:P_BF + 172] = _fm(np.asarray(b_dw_f[0], f), 172)
    par[:, P_GPO:P_GPO + 32] = _fm(np.asarray(g_post_ffn[0], f), 32)

    sca = np.asarray(state_conv_a[0], f)
    scb = np.asarray(state_conv_b[0], f)
    slr = np.asarray(state_lru[0], f)
    sff = np.asarray(state_ffn[0], f)
    in_maps = []
    for c in range(NCORES):
        s, hf = c // 2, c % 2
        xm = np.zeros((2, KB, 128, E), f)
        xpre = np.zeros((2, KB, 128, TP), f)
        for hi in range(2):
            p0 = hf * 1024 + hi * 512
            if p0 > 0:
                xm[hi, :, :, 0:32] = _tok_fm(x_prompt[s, p0 - 32:p0])
            xm[hi, :, :, 32:544] = _tok_fm(x_prompt[s, p0:p0 + 512])
            q0 = 4 * c + 2 * hi
            xm[hi, :, :, 544:576] = _tok_fm(x_sample[q0:q0 + 2].reshape(32, D))
        if hf == 1:
            for r in range(2):
                xpre[r] = _tok_fm(x_prompt[s, r * 512:(r + 1) * 512])
        msk = np.zeros((128, 2), f)
        msk[:, 0] = float(hf)
        msk[:, 1] = 1.0
        sa = np.zeros((2, 128, 960), f)
        sbv = np.zeros((128, 2, 16, 2, 3), f)
        sh = np.zeros((128, 2, 16, 2), f)
        sf = np.zeros((128, 2, 172, 2, 2), f)
        for hi in range(2):
            q0 = 4 * c + 2 * hi
            sa[hi] = sca[q0:q0 + 2].reshape(2, 30, 16, 128).transpose(3, 2, 0, 1).reshape(128, 960)
            sbv[:, hi] = scb[q0:q0 + 2].reshape(2, 3, 16, 128).transpose(3, 2, 0, 1)
            sh[:, hi] = slr[q0:q0 + 2].reshape(2, 16, 128).transpose(2, 1, 0)
            sf[:, hi] = sff[q0:q0 + 2].reshape(2, 2, 172, 128).transpose(3, 2, 0, 1)
        in_maps.append({"xm": xm, "xp": xpre, "par": par, "msk": msk, "wg": wg, "wall": wall,
                        "sa": sa, "sbv": sbv.reshape(128, 192), "sh": sh.reshape(128, 64),
                        "sf": sf.reshape(128, 2 * 172 * 4)})
    if "nc" not in _NC_CACHE:
        _NC_CACHE["nc"] = build_program()
    nc, plan = _NC_CACHE["nc"]
    nch = max(c for c, _ in plan) + 1 if plan else 1
    if nch < NCH:
        wall = np.ascontiguousarray(wall[:nch])
        for m_ in in_maps:
            m_["wall"] = wall
    res = run_bass_kernel_spmd(nc, in_maps, core_ids=list(range(NCORES)))
    R = res.results
    yp = np.zeros((4, 2048, D), f)
    ys = np.zeros((32, 16, D), f)
    pa = np.zeros((1, 4, 30, 2048), f)
    pb = np.zeros((1, 4, 3, 2048), f)
    ph = np.zeros((1, 4, 2048), f)
    pf = np.zeros((1, 4, 2, 22016), f)
    sa_o = np.zeros((1, 32, 30, 2048), f)
    sb_o = np.zeros((1, 32, 3, 2048), f)
    sh_o = np.zeros((1, 32, 2048), f)
    sf_o = np.zeros((1, 32, 2, 22016), f)
    for c in range(NCORES):
        s, hf = c // 2, c % 2
        r = R[c]
        y = np.asarray(r["y"]).reshape(2, D, NB)
        oca = np.asarray(r["oca"]).reshape(2, 2048, 3, 30)
        ocb = np.asarray(r["ocb"]).reshape(128, 2, 16, 3, 3)
        olr = np.asarray(r["olr"]).reshape(128, 2, 16, 3)
        off = np.asarray(r["off"]).reshape(128, 2, 172, 3, 2)
        for hi in range(2):
            p0 = hf * 1024 + hi * 512
            yp[s, p0:p0 + 512] = y[hi, :, 0:512].T
            q0 = 4 * c + 2 * hi
            ys[q0:q0 + 2] = y[hi, :, 512:544].T.reshape(2, 16, D)
            ocb_h = ocb[:, hi].transpose(2, 3, 1, 0).reshape(3, 3, 2048)
            olr_h = olr[:, hi].transpose(2, 1, 0).reshape(3, 2048)
            off_h = off[:, hi].transpose(2, 3, 1, 0).reshape(3, 2, 22016)
            oca_h = oca[hi].transpose(1, 2, 0)
            for q in range(2):
                sa_o[0, q0 + q] = oca_h[1 + q]
                sb_o[0, q0 + q] = ocb_h[1 + q]
                sh_o[0, q0 + q] = olr_h[1 + q]
                sf_o[0, q0 + q] = off_h[1 + q]
            if hf == 1 and hi == 1:
                pa[0, s] = oca_h[0]
                pb[0, s] = ocb_h[0]
                ph[0, s] = olr_h[0]
                pf[0, s] = off_h[0]
    return (yp, ys, pa, pb, ph, pf, sa_o, sb_o, sh_o, sf_o)
```
